# Optimizing a Trainium2 kernel written in Bass

```python
import math
import jax, jax.numpy as jnp
from jax import lax
import numpy as np

D_MODEL = 1024
BATCH = 8
SEQ = 2048
DEPTH = 2

D_MIX = D_MODEL
DN_HEADS = 4
DN_DK = 128
DN_DV = 128
DN_CONV = 4
DN_CHUNK = 64
DN_WIDTH = DN_HEADS * DN_DV
DN_KEY_WIDTH = DN_HEADS * DN_DK
GLA_HEADS = 4
GLA_DK = 64
GLA_DV = 128
GLA_GATE_RANK = 16
GLA_TAU = 16.0
GLA_CHUNK = 16
GLA_WIDTH = GLA_HEADS * GLA_DV
GLA_KEY_WIDTH = GLA_HEADS * GLA_DK
IN_SIZES = (DN_KEY_WIDTH, DN_KEY_WIDTH, DN_WIDTH,
            DN_WIDTH,
            DN_HEADS, DN_HEADS,
            GLA_KEY_WIDTH, GLA_KEY_WIDTH, GLA_WIDTH,
            GLA_WIDTH,
            GLA_GATE_RANK)
D_IN_PROJ = 3 * 512 + 512 + 4 + 4 + 256 + 256 + 512 + 512 + 16
DEEPNORM_ALPHA = (2.0 * DEPTH) ** 0.25
DEEPNORM_BETA = (8.0 * DEPTH) ** -0.25
NORM_EPS = 1e-6

kernel_name = "hymba_style_gdn_gla_deepnorm"


def _split(a, sizes):
    out, off = [], 0
    for s in sizes:
        out.append(a[..., off:off + s])
        off += s
    return out


def _to_chunks(a, chunk):
    b, t = a.shape[:2]
    a = a.reshape((b, t // chunk, chunk) + a.shape[2:])
    return jnp.moveaxis(a, 3, 1)


def _from_scan(o):
    n, b, h, c, d = o.shape
    return jnp.transpose(o, (1, 0, 3, 2, 4)).reshape(b, n * c, h, d)


def causal_short_conv(x, w):
    k, c = w.shape
    return lax.conv_general_dilated(
        x, w[:, None, :].astype(x.dtype), window_strides=(1,),
        padding=[(k - 1, 0)], dimension_numbers=("NWC", "WIO", "NWC"),
        feature_group_count=c)


def l2norm(x):
    xf = x.astype(jnp.float32)
    return xf * lax.rsqrt(jnp.sum(xf * xf, axis=-1, keepdims=True) + NORM_EPS)


def head_rms_norm(o, g):
    of = o.astype(jnp.float32)
    of = of * lax.rsqrt(jnp.mean(of * of, axis=-1, keepdims=True) + NORM_EPS)
    return of * g.astype(jnp.float32)


def layer_norm(x, g, b):
    xf = x.astype(jnp.float32)
    mu = jnp.mean(xf, axis=-1, keepdims=True)
    var = jnp.mean(jnp.square(xf - mu), axis=-1, keepdims=True)
    y = (xf - mu) * lax.rsqrt(var + NORM_EPS) * g.astype(jnp.float32) + b.astype(jnp.float32)
    return y.astype(x.dtype)


def gated_delta_rule_chunked(q, k, v, g, beta):
    f32 = jnp.float32
    q, k, v, g, beta = (a.astype(f32) for a in (q, k, v, g, beta))
    b_, t, h, dk = q.shape
    dv = v.shape[-1]
    c = DN_CHUNK
    qc, kc, vc = (_to_chunks(a, c) for a in (q, k, v))
    gc, bc = _to_chunks(g, c), _to_chunks(beta, c)
    G = jnp.cumsum(gc, axis=-1)
    pos = jnp.arange(c)
    causal = pos[:, None] >= pos[None, :]
    strict = pos[:, None] > pos[None, :]
    diff = G[..., :, None] - G[..., None, :]
    decay = jnp.where(causal, jnp.exp(jnp.where(causal, diff, 0.0)), 0.0)
    k_beta = kc * bc[..., None]
    kk = jnp.einsum("bhnid,bhnjd->bhnij", k_beta, kc)
    lower = jnp.eye(c, dtype=f32) + jnp.where(strict, kk * decay, 0.0)
    rhs = jnp.concatenate([vc * bc[..., None], k_beta * jnp.exp(G)[..., None]], axis=-1)
    uw = lax.linalg.triangular_solve(lower, rhs, left_side=True, lower=True)
    u, w = uw[..., :dv], uw[..., dv:]
    attn = jnp.einsum("bhnid,bhnjd->bhnij", qc, kc) * decay
    xs = tuple(jnp.moveaxis(a, 2, 0) for a in (qc, kc, u, w, G, attn))
    s0 = jnp.zeros((b_, h, dk, dv), f32)

    def step(S, inp):
        q_n, k_n, u_n, w_n, g_n, a_n = inp
        v_new = u_n - jnp.einsum("bhid,bhde->bhie", w_n, S)
        o_n = (jnp.einsum("bhid,bhde->bhie", q_n * jnp.exp(g_n)[..., None], S)
               + jnp.einsum("bhij,bhje->bhie", a_n, v_new))
        g_last = g_n[..., -1]
        k_dec = k_n * jnp.exp(g_last[..., None] - g_n)[..., None]
        S = S * jnp.exp(g_last)[..., None, None] + jnp.einsum("bhid,bhie->bhde", k_dec, v_new)
        return S, o_n

    _, o = lax.scan(step, s0, xs)
    return _from_scan(o)


def gla_chunked(q, k, v, gk):
    f32 = jnp.float32
    q, k, v, gk = (a.astype(f32) for a in (q, k, v, gk))
    b_, t, h, dk = q.shape
    dv = v.shape[-1]
    c = GLA_CHUNK
    qc, kc, vc, gc = (_to_chunks(a, c) for a in (q, k, v, gk))
    Bc = jnp.cumsum(gc, axis=-2)
    pos = jnp.arange(c)
    causal = (pos[:, None] >= pos[None, :])[..., None]
    diff = Bc[..., :, None, :] - Bc[..., None, :, :]
    dec = jnp.where(causal, jnp.exp(jnp.where(causal, diff, 0.0)), 0.0)
    attn = jnp.einsum("bhnid,bhnjd,bhnijd->bhnij", qc, kc, dec)
    xs = tuple(jnp.moveaxis(a, 2, 0) for a in (qc, kc, vc, Bc, attn))
    s0 = jnp.zeros((b_, h, dk, dv), f32)

    def step(S, inp):
        q_n, k_n, v_n, b_n, a_n = inp
        o_n = (jnp.einsum("bhid,bhde->bhie", q_n * jnp.exp(b_n), S)
               + jnp.einsum("bhij,bhje->bhie", a_n, v_n))
        b_last = b_n[..., -1:, :]
        k_dec = k_n * jnp.exp(b_last - b_n)
        S = S * jnp.exp(b_last[..., 0, :])[..., None] + jnp.einsum("bhid,bhie->bhde", k_dec, v_n)
        return S, o_n

    _, o = lax.scan(step, s0, xs)
    return _from_scan(o)


def hybrid_layer(x, w_in, conv_w, dn_a_log, dn_dt_bias, gla_gate_w2, gla_gate_b,
                 dn_norm_g, gla_norm_g, w_out, ln_g, ln_b):
    bsz, t, _ = x.shape
    proj = jnp.einsum("btd,de->bte", x, w_in)
    (dn_q, dn_k, dn_v, dn_z, dn_b, dn_a,
     g_q, g_k, g_v, g_z, g_r) = _split(proj, IN_SIZES)

    dn_qkv = jax.nn.silu(causal_short_conv(jnp.concatenate([dn_q, dn_k, dn_v], axis=-1), conv_w))
    cq, ck, cv = _split(dn_qkv, (DN_KEY_WIDTH, DN_KEY_WIDTH, DN_WIDTH))
    q = l2norm(cq.reshape(bsz, t, DN_HEADS, DN_DK)) * (DN_DK ** -0.5)
    k = l2norm(ck.reshape(bsz, t, DN_HEADS, DN_DK))
    v = cv.reshape(bsz, t, DN_HEADS, DN_DV)
    beta = jax.nn.sigmoid(dn_b.astype(jnp.float32))
    g = -jnp.exp(dn_a_log.astype(jnp.float32)) * jax.nn.softplus(
        dn_a.astype(jnp.float32) + dn_dt_bias.astype(jnp.float32))
    o_dn = gated_delta_rule_chunked(q, k, v, g, beta)
    o_dn = head_rms_norm(o_dn, dn_norm_g) * jax.nn.silu(
        dn_z.reshape(bsz, t, DN_HEADS, DN_DV).astype(jnp.float32))

    gq = g_q.reshape(bsz, t, GLA_HEADS, GLA_DK) * (GLA_DK ** -0.5)
    gk = g_k.reshape(bsz, t, GLA_HEADS, GLA_DK)
    gv = g_v.reshape(bsz, t, GLA_HEADS, GLA_DV)
    gate_logits = jnp.einsum("btr,re->bte", g_r, gla_gate_w2) + gla_gate_b
    log_f = (jax.nn.log_sigmoid(gate_logits.astype(jnp.float32)) / GLA_TAU).reshape(
        bsz, t, GLA_HEADS, GLA_DK)
    o_gla = gla_chunked(gq, gk, gv, log_f)
    o_gla = head_rms_norm(o_gla, gla_norm_g) * jax.nn.silu(
        g_z.reshape(bsz, t, GLA_HEADS, GLA_DV).astype(jnp.float32))

    o = jnp.concatenate([o_dn.reshape(bsz, t, DN_WIDTH),
                         o_gla.reshape(bsz, t, GLA_WIDTH)], axis=-1).astype(x.dtype)
    y = jnp.einsum("bte,ed->btd", o, w_out)
    return layer_norm(DEEPNORM_ALPHA * x + y, ln_g, ln_b)


def setup_inputs(seed: int = 0) -> dict:
    key = jax.random.key(seed)
    ks = jax.random.split(key, 12)
    f32 = jnp.float32
    x = jax.random.normal(ks[0], (BATCH, SEQ, D_MODEL), f32)
    col_scale = jnp.concatenate([
        jnp.ones((2 * DN_KEY_WIDTH,), f32),
        jnp.full((DN_WIDTH,), DEEPNORM_BETA, f32),
        jnp.ones((DN_WIDTH + 2 * DN_HEADS + 2 * GLA_KEY_WIDTH,), f32),
        jnp.full((GLA_WIDTH,), DEEPNORM_BETA, f32),
        jnp.ones((GLA_WIDTH + GLA_GATE_RANK,), f32)])
    w_in = jax.random.normal(ks[1], (DEPTH, D_MODEL, D_IN_PROJ), f32) * (D_MODEL ** -0.5) * col_scale
    conv_w = jax.random.normal(ks[2], (DEPTH, DN_CONV, DN_KEY_WIDTH * 2 + DN_WIDTH), f32) * (DN_CONV ** -0.5)
    dn_a_log = jnp.log(jax.random.uniform(ks[3], (DEPTH, DN_HEADS), f32, 1.0, 16.0))
    dt = jnp.exp(jax.random.uniform(ks[4], (DEPTH, DN_HEADS), f32, math.log(1e-3), math.log(1e-1)))
    dn_dt_bias = dt + jnp.log(-jnp.expm1(-dt))
    gla_gate_w2 = jax.random.normal(ks[5], (DEPTH, GLA_GATE_RANK, GLA_KEY_WIDTH), f32) * (GLA_GATE_RANK ** -0.5)
    gla_gate_b = 0.1 * jax.random.normal(ks[6], (DEPTH, GLA_KEY_WIDTH), f32)
    dn_norm_g = 1.0 + 0.02 * jax.random.normal(ks[7], (DEPTH, DN_DV), f32)
    gla_norm_g = 1.0 + 0.02 * jax.random.normal(ks[8], (DEPTH, GLA_DV), f32)
    w_out = jax.random.normal(ks[9], (DEPTH, D_MIX, D_MODEL), f32) * (D_MIX ** -0.5) * DEEPNORM_BETA
    ln_g = 1.0 + 0.02 * jax.random.normal(ks[10], (DEPTH, D_MODEL), f32)
    ln_b = 0.02 * jax.random.normal(ks[11], (DEPTH, D_MODEL), f32)
    return {"x": x, "w_in": w_in, "conv_w": conv_w, "dn_a_log": dn_a_log,
            "dn_dt_bias": dn_dt_bias, "gla_gate_w2": gla_gate_w2, "gla_gate_b": gla_gate_b,
            "dn_norm_g": dn_norm_g, "gla_norm_g": gla_norm_g, "w_out": w_out,
            "ln_g": ln_g, "ln_b": ln_b}


def reference(x, w_in, conv_w, dn_a_log, dn_dt_bias, gla_gate_w2, gla_gate_b,
              dn_norm_g, gla_norm_g, w_out, ln_g, ln_b):
    h = x
    for l in range(DEPTH):
        h = hybrid_layer(h, w_in[l], conv_w[l], dn_a_log[l], dn_dt_bias[l],
                         gla_gate_w2[l], gla_gate_b[l], dn_norm_g[l], gla_norm_g[l],
                         w_out[l], ln_g[l], ln_b[l])
    return h
```

```python
import contextlib
import numpy as np
import concourse.bass as bass
import concourse.mybir as mybir
from concourse.bass_utils import run_bass_kernel_spmd

F32 = mybir.dt.float32
BF16 = mybir.dt.bfloat16
ALU = mybir.AluOpType
AF = mybir.ActivationFunctionType

T = 2048
D = 1024
NT = 16
KC = 8
DEPTH = 2
D_IN = 3608
ALPHA = (2.0 * DEPTH) ** 0.25
EPS = 1e-6
NEG = -30000.0

NDMA = 12


def _region(ap):
    sp = str(ap.space).upper()
    if "DRAM" in sp:
        return None
    if "PSUM" in sp:
        return (ap.tensor.name, 0, 128, 0, 1 << 30)
    pat = ap.ap
    pstride, pn = pat[0]
    off = int(ap.offset)
    if pstride == 0:
        p_lo, f_lo, pn = 0, off, 1
    else:
        p_lo, f_lo = off // pstride, off % pstride
    ext = 1
    for st, cnt in pat[1:]:
        ext += abs(st) * (cnt - 1)
    esz = mybir.dt.size(ap.dtype)
    return (ap.tensor.name, p_lo, p_lo + pn, f_lo * esz, (f_lo + ext) * esz)


class Sched:
    ENG = ("pe", "dve", "act", "pool", "sp")

    def __init__(self, nc):
        self.nc = nc
        self.prog = {e: [] for e in self.ENG}
        self.cnt = {e: 0 for e in self.ENG}
        self.seen = {e: {} for e in self.ENG}
        self.dcnt = [0] * NDMA
        self.dnext = 0
        self.recs = {}
        self.out_dmas = []

    def _deps(self, reads, writes):
        deps = {}
        for lst, only_w in ((reads, True), (writes, False)):
            for ap in lst:
                r = _region(ap)
                if r is None:
                    continue
                for rec in self.recs.get(r[0], ()):
                    if only_w and not rec[6]:
                        continue
                    if rec[0] < r[2] and r[1] < rec[1] and rec[2] < r[4] and r[3] < rec[3]:
                        if deps.get(rec[4], 0) < rec[5]:
                            deps[rec[4]] = rec[5]
        return deps

    def _record(self, reads, writes, src, val):
        for ap in writes:
            r = _region(ap)
            if r is None:
                continue
            lst = self.recs.setdefault(r[0], [])
            lst[:] = [x for x in lst if not (r[1] <= x[0] and x[1] <= r[2] and r[3] <= x[2] and x[3] <= r[4])]
            lst.append([r[1], r[2], r[3], r[4], src, val, True])
        for ap in reads:
            r = _region(ap)
            if r is None:
                continue
            lst = self.recs.setdefault(r[0], [])
            for x in lst:
                if (not x[6]) and x[4] == src and x[0] == r[1] and x[1] == r[2] and x[2] == r[3] and x[3] == r[4]:
                    x[5] = max(x[5], val)
                    break
            else:
                lst.append([r[1], r[2], r[3], r[4], src, val, False])

    def _waits(self, eng, deps):
        out = []
        seen = self.seen[eng]
        for src, val in deps.items():
            if src == eng and eng == "pe":
                continue
            if seen.get(src, 0) >= val:
                continue
            seen[src] = val
            out.append((src, val))
        return out

    @staticmethod
    def _excl(reads, writes):
        ps = [a for a in reads if "PSUM" in str(a.space).upper()]
        if ps:
            reads = [a for a in reads if "PSUM" not in str(a.space).upper()]
            writes = list(writes) + ps
        return reads, writes

    def op(self, eng, fn, reads=(), writes=()):
        reads, writes = self._excl(list(reads), list(writes))
        deps = self._deps(reads, writes)
        idx = self.cnt[eng] + 1
        self.cnt[eng] = idx
        waits = self._waits(eng, deps)
        self.prog[eng].append((waits, fn, (eng, 1)))
        self._record(reads, writes, eng, idx)
        return idx

    def dma(self, queue, out, in_, **kw):
        deps = self._deps([in_], [out])
        c = self.dnext
        self.dnext = (self.dnext + 1) % NDMA
        src = "dma%d" % c
        if self.dcnt[c] > 0 and deps.get(src, 0) < 16 * self.dcnt[c]:
            deps[src] = 16 * self.dcnt[c]
        self.dcnt[c] += 1
        val = 16 * self.dcnt[c]
        waits = self._waits(queue, deps)

        def fn(e, out=out, in_=in_, kw=kw):
            return e.dma_start(out=out, in_=in_, **kw)

        self.prog[queue].append((waits, fn, (src, 16)))
        self._record([in_], [out], src, val)
        if _region(out) is None:
            self.out_dmas.append((src, val))

    def emit(self):
        nc = self.nc
        with contextlib.ExitStack() as st:
            sems = {}
            for e in self.ENG:
                sems[e] = st.enter_context(nc.semaphore("s_" + e))
            for c in range(NDMA):
                sems["dma%d" % c] = st.enter_context(nc.semaphore("s_dma%d" % c))
            fin = {}
            for src, val in self.out_dmas:
                fin[src] = max(fin.get(src, 0), val)
            block = st.enter_context(nc.Block())
            handles = {"pe": "tensor", "dve": "vector", "act": "scalar", "pool": "gpsimd", "sp": "sync"}

            def make(ename):
                prog = self.prog[ename]

                def body(e):
                    for waits, fn, (isrc, inc) in prog:
                        for src, val in waits:
                            e.wait_ge(sems[src], val)
                        fn(e).then_inc(sems[isrc], inc)
                    if ename == "sp":
                        for src, val in fin.items():
                            e.wait_ge(sems[src], val)
                return body

            for ename in self.ENG:
                getattr(block, handles[ename])(make(ename))


class _Stop(Exception):
    pass


def build(layers, debug=(), stop=None):
    nc = bass.Bass("TRN2", target_bir_lowering=False)
    dram = {}

    def din(name, shape):
        dram[name] = nc.dram_tensor(name, list(shape), F32, kind="ExternalInput").ap()
        return dram[name]

    x_d = din("x", [T, D])
    w_in_d = din("w_in", [DEPTH, D, D_IN])
    conv_d = din("conv_wt", [DEPTH, 1536, 4])
    alog_d = din("dn_a_log", [DEPTH, 4])
    dtb_d = din("dn_dt_bias", [DEPTH, 4])
    w2_d = din("gla_gate_w2", [DEPTH, 16, 256])
    gb_d = din("gla_gate_b", [DEPTH, 256])
    dng_d = din("dn_norm_g", [DEPTH, 128])
    glg_d = din("gla_norm_g", [DEPTH, 128])
    w_out_d = din("w_out", [DEPTH, D, D])
    lng_d = din("ln_g", [DEPTH, D])
    lnb_d = din("ln_b", [DEPTH, D])
    y_d = nc.dram_tensor("y", [T, D], F32, kind="ExternalOutput").ap()
    dbg_out = {}

    with contextlib.ExitStack() as st:
        def sb(name, shape, dt=F32):
            return st.enter_context(nc.sbuf_tensor(name, list(shape), dt))

        def ps(name, shape, dt=F32):
            return st.enter_context(nc.psum_tensor(name, list(shape), dt))

        s = Sched(nc)

        x_tm = sb("x_tm", [128, NT, D])
        xT = sb("xT", [128, KC, T], BF16)
        oT_all = sb("oT_all", [128, 4, T], BF16)
        wst = [sb("wst%d" % i, [128, KC, 128]) for i in range(2)]
        wbf = [sb("wbf%d" % i, [128, KC, 128], BF16) for i in range(2)]
        raw = sb("raw", [128, 3 + T])
        acc = sb("acc", [128, T])
        qk = sb("qk", [128, 2, T], BF16)
        vT = sb("vT", [128, T], BF16)
        zs = sb("zs", [128, T], BF16)
        ident = sb("ident", [128, 128])
        ident_bf = sb("ident_bf", [128, 128], BF16)
        maskLE = sb("maskLE", [128, 128])
        maskGT = sb("maskGT", [128, 128])
        ones_f = sb("ones_f", [128, 128])
        ones_bf = sb("ones_bf", [128, 128], BF16)
        negSL = sb("negSL", [128, 128], BF16)
        negUI = sb("negUI", [128, 128], BF16)
        m01UI = sb("m01UI", [128, 128])
        convw = sb("convw", [128, 12, 4])
        alog_b = sb("alog_b", [128, 4])
        dtb_b = sb("dtb_b", [128, 4])
        nA_b = sb("nA_b", [128, 4])
        w2_sb = sb("w2_sb", [16, 256], BF16)
        gb_sb = sb("gb_sb", [1, 256])
        dng = sb("dng", [128, 1])
        glg = sb("glg", [128, 1])
        w8st = sb("w8st", [128, KC, 24])
        w8bf = sb("w8bf", [128, KC, 24], BF16)
        grT = sb("grT", [16, T], BF16)
        beta = sb("beta", [128, NT, 4])
        gsb = sb("gsb", [128, NT, 4])
        G_sb = sb("G_sb", [128, NT, 4])
        eG = sb("eG", [128, NT, 4])
        bG = sb("bG", [128, NT, 4])
        eGl = sb("eGl", [128, NT, 4])
        ekd = sb("ekd", [128, NT, 4])
        tmp64 = sb("tmp64", [128, NT, 4])
        NS = 2
        GT = 4
        L2 = sb("L2g", [128, GT, 128])
        L1 = sb("L1g", [128, GT, 128])
        E_sb = L2
        eGB = L1
        Pg = sb("Pg", [128, GT, 128])
        ET_sb = Pg
        CB = [sb("CB%d" % j, [128, 2, 2, GT // 2, 128]) for j in range(2)]
        TT_bf = sb("TTg", [128, GT, 128], BF16)
        kbg = sb("kbg", [128, GT, 128], BF16)
        kdec2 = [sb("kdec%d" % i, [128, GT, 128], BF16) for i in range(2)]
        vb = sb("vbg", [128, GT, 128], BF16)
        attnT2 = [sb("attnTg%d" % i, [128, GT, 128], BF16) for i in range(2)]
        wT_sb = sb("wTg", [128, GT, 128], BF16)
        u_sb = sb("ug", [128, GT, 128], BF16)
        vnew = [sb("vnew_%d" % i, [128, 128], BF16) for i in range(NS)]
        sq2 = sb("sq2", [128, 512], BF16)
        S32 = sb("S32", [128, 128])
        S_bf = sb("S_bf", [128, 128], BF16)
        sq = sb("sq", [128, 512], BF16)
        scr = sb("scr", [128, 512])
        rk = scr[:, 0:512]
        ebuf = [L1[0:64].rearrange("p g c -> p (g c)"), L2[0:64].rearrange("p g c -> p (g c)")]
        rawr = sb("rawr", [128, 4 + T], mybir.dt.float32r)
        dgw = sb("dgw", [128, 4, 128], mybir.dt.float32r)
        stat = sb("stat", [128, 64])

        pj = [ps("pj%d" % i, [128, 512]) for i in range(2)]
        pa = ps("pa", [128, 512])
        pb = ps("pb", [128, 512])
        pt = ps("pt", [128, 1024], BF16)
        pc = ps("pc", [128, 512])
        pr = ps("pr", [128, 512])
        po = ps("po", [128, 512])

        def mm(out, lhsT, rhs, start=True, stop=True):
            rd = [lhsT, rhs] + ([] if start else [out])
            s.op("pe", lambda e: e.matmul(out, lhsT=lhsT, rhs=rhs, start=start, stop=stop), reads=rd, writes=[out])

        def tr(out, in_, idn):
            s.op("pe", lambda e: e.transpose(out, in_, idn), reads=[in_, idn], writes=[out])

        def act(out, in_, func, bias=None, scale=None, accum=None):
            kw = {}
            rd = [in_]
            if bias is not None:
                kw["bias"] = bias
                if not isinstance(bias, float):
                    rd.append(bias)
            if scale is not None:
                kw["scale"] = scale
                if not isinstance(scale, float):
                    rd.append(scale)
            wr = [out]
            if accum is not None:
                kw["accum_out"] = accum
                wr.append(accum)
            s.op("act", lambda e: e.activation(out=out, in_=in_, func=func, **kw), reads=rd, writes=wr)

        def tt(eng, out, in0, in1, op):
            s.op(eng, lambda e: e.tensor_tensor(out=out, in0=in0, in1=in1, op=op), reads=[in0, in1], writes=[out])

        def ts(eng, out, in0, s1, op0, s2=None, op1=None):
            rd = [in0] + [x for x in (s1, s2) if x is not None and not isinstance(x, float)]
            if op1 is None:
                s.op(eng, lambda e: e.tensor_scalar(out=out, in0=in0, scalar1=s1, scalar2=None, op0=op0), reads=rd, writes=[out])
            else:
                s.op(eng, lambda e: e.tensor_scalar(out=out, in0=in0, scalar1=s1, scalar2=s2, op0=op0, op1=op1), reads=rd, writes=[out])

        def stt(out, in0, sc, in1, op0, op1):
            rd = [in0, in1] + ([] if isinstance(sc, float) else [sc])
            s.op("dve", lambda e: e.scalar_tensor_tensor(out=out, in0=in0, scalar=sc, in1=in1, op0=op0, op1=op1), reads=rd, writes=[out])

        def cp(eng, out, in_):
            if eng == "act":
                s.op("act", lambda e: e.copy(out=out, in_=in_), reads=[in_], writes=[out])
            else:
                s.op(eng, lambda e: e.tensor_copy(out=out, in_=in_), reads=[in_], writes=[out])

        def memset(ap, v):
            s.op("pool", lambda e: e.memset(ap, v), writes=[ap])

        def asel(out, in_, base, cm, step, cmp, fill):
            s.op("pool", lambda e: e.affine_select(out=out, in_=in_, pattern=[[step, 128]], base=base,
                                                   channel_multiplier=cm, compare_op=cmp, fill=fill),
                 reads=[in_], writes=[out])

        def dbg(name, ap):
            if stop == name:
                raise _Stop()
            if name in debug:
                shp = list(ap.shape)
                d = nc.dram_tensor("dbg_" + name, shp, ap.dtype, kind="ExternalOutput").ap()
                dbg_out[name] = d
                s.dma("sp", d, ap)

        wstate = {"st": 0, "bf": 0}

        def load_w(src_ap, ncols):
            a = wst[wstate["st"] % 2]
            b = wbf[wstate["bf"] % 2]
            wstate["st"] += 1
            wstate["bf"] += 1
            s.dma("sp", a[:, :, 0:ncols], src_ap)
            cp("pool", b[:, :, 0:ncols], a[:, :, 0:ncols])
            return b

        def proj_fm(wb, M, evac):
            for tb in range(4):
                p = pj[tb % 2]
                for k in range(KC):
                    mm(p[0:M, :], wb[:, k, 0:M], xT[:, k, tb * 512:(tb + 1) * 512], start=(k == 0), stop=(k == KC - 1))
                evac(tb, p[0:M, :])

        def ln_stats(t0, n):
            st_ = stat[:, ((t0 // n) % 2) * 32:((t0 // n) % 2) * 32 + 32].rearrange("p (a j) -> p a j", j=4)
            zo = raw[:, 3:3 + D]
            for j in range(n):
                z = x_tm[:, t0 + j, :]
                act(zo, z, AF.Identity, accum=st_[:, 0, j:j + 1])
                act(zo, z, AF.Square, accum=st_[:, 1, j:j + 1])
            ts("dve", st_[:, 2, 0:n], st_[:, 0, 0:n], 1.0 / D, ALU.mult)
            tt("dve", st_[:, 3, 0:n], st_[:, 2, 0:n], st_[:, 2, 0:n], ALU.mult)
            stt(st_[:, 4, 0:n], st_[:, 1, 0:n], 1.0 / D, st_[:, 3, 0:n], ALU.mult, ALU.subtract)
            act(st_[:, 5, 0:n], st_[:, 4, 0:n], AF.Ln, bias=EPS)
            act(st_[:, 5, 0:n], st_[:, 5, 0:n], AF.Exp, scale=-0.5)
            stt(st_[:, 6, 0:n], st_[:, 2, 0:n], -1.0, st_[:, 5, 0:n], ALU.mult, ALU.mult)

        def ln_apply(t0, n, lng_t, lnb_t, last):
            st_ = stat[:, ((t0 // n) % 2) * 32:((t0 // n) % 2) * 32 + 32].rearrange("p (a j) -> p a j", j=4)
            for j in range(n):
                z = x_tm[:, t0 + j, :]
                act(z, z, AF.Identity, bias=st_[:, 6, j:j + 1], scale=st_[:, 5, j:j + 1])
                tt("dve", z, z, lng_t, ALU.mult)
                tt("pool", z, z, lnb_t, ALU.add)
                if last:
                    s.dma("sp", y_d[(t0 + j) * 128:(t0 + j + 1) * 128, :], z)

        def proj_steps(wb, M, evac):
            for tb in range(4):
                p = pj[tb % 2]
                for k in range(KC):
                    mm(p[0:M, :], wb[:, k, 0:M], xT[:, k, tb * 512:(tb + 1) * 512], start=(k == 0), stop=(k == KC - 1))
                evac(tb, p[0:M, :])
                yield

        def interleave(*gens):
            gens = list(gens)
            while gens:
                for g_ in list(gens):
                    try:
                        next(g_)
                    except StopIteration:
                        gens.remove(g_)

        def rms_steps(src, ln_scale, out_fn, rkbufs):
            sqb = (sq[:], sq2[:])
            pbk = (pa, pb)

            def stage1(tb):
                act(sqb[tb % 2], src(tb), AF.Square)
                mm(pbk[tb % 2][:, :], ones_bf[:], sqb[tb % 2])

            def stage2(tb):
                rkb = rkbufs[tb % 2]
                act(rkb, pbk[tb % 2][:, :], AF.Ln, bias=EPS, scale=ln_scale)
                act(rkb, rkb, AF.Exp, scale=-0.5)
                out_fn(tb, rkb)
            stage1(0)
            stage1(1)
            yield
            for tb in range(4):
                stage2(tb)
                if tb + 2 < 4:
                    stage1(tb + 2)
                yield

        def rms_blocks(src, ln_scale, out_fn, rkbufs):
            for _ in rms_steps(src, ln_scale, out_fn, rkbufs):
                pass

        def outproj_w(rh, buf2d):
            wob = buf2d.rearrange("p (k c) -> p k c", k=4)
            for cc in range(8):
                c0 = cc * 128
                a = wst[wstate["st"] % 2]
                wstate["st"] += 1
                s.dma("sp", a[:, 0:4, :], cur["w_out_l"][:, rh * 4:(rh + 1) * 4, c0:c0 + 128])
                cp("pool", wob[:, :, c0:c0 + 128], a[:, 0:4, :])
            return wob

        def outproj(rh, first, ln=None):
            wob = cur.get("wob")
            if wob is None:
                wob = outproj_w(rh, qk[:].rearrange("p a t -> p (a t)"))
            cur["wob"] = None
            for t in range(NT):
                for half in range(2):
                    p = pj[half]
                    for e4 in range(4):
                        mm(p[:, :], oT_all[:, e4, t * 128:(t + 1) * 128], wob[:, e4, half * 512:(half + 1) * 512],
                           start=(e4 == 0), stop=(e4 == 3))
                    xs = x_tm[:, t, half * 512:(half + 1) * 512]
                    if first:
                        stt(xs, xs, ALPHA, p[:, :], ALU.mult, ALU.add)
                    else:
                        tt("dve", xs, xs, p[:, :], ALU.add)
                if ln is not None and t % 4 == 3:
                    ln_stats(t - 3, 4)
                    if t >= 7:
                        ln_apply(t - 7, 4, *ln)
            if ln is not None:
                ln_apply(NT - 4, 4, *ln)

        cur = {}
        memset(ones_f[:], 1.0)
        memset(ones_bf[:], 1.0)
        asel(ident[:], ones_f[:], 0, -1, 1, ALU.is_equal, 0.0)
        cp("pool", ident_bf[:], ident[:])
        asel(maskLE[:], ones_f[:], 0, -1, 1, ALU.is_ge, 0.0)
        asel(maskGT[:], ones_f[:], 0, 1, -1, ALU.is_gt, 0.0)
        asel(m01UI[:], ones_f[:], 0, -1, 1, ALU.is_ge, 0.0)
        memset(L1[:, 0, :], 0.0)
        asel(L2[:, 0, :], L1[:, 0, :], 0, 1, -1, ALU.is_gt, NEG)
        cp("pool", negSL[:], L2[:, 0, :])
        asel(L2[:, 1, :], L1[:, 0, :], 0, -1, 1, ALU.is_ge, NEG)
        cp("pool", negUI[:], L2[:, 1, :])
        act(rawr[:, 0:3], L1[:, 0, 0:3], AF.Copy)

        for t in range(NT):
            s.dma("sp", x_tm[:, t, :], x_d[t * 128:(t + 1) * 128, :])

        for li, l in enumerate(layers):
          try:
            last = (li == len(layers) - 1)
            w_in_l = w_in_d[l].rearrange("(k p) c -> p k c", p=128)
            w_out_l = w_out_d[l].rearrange("(k p) c -> p k c", p=128)
            cur["w_out_l"] = w_out_l

            s.dma("sp", convw[:], conv_d[l].rearrange("(n p) j -> p n j", p=128))
            s.dma("sp", alog_b[:], alog_d[l].partition_broadcast(128))
            s.dma("sp", dtb_b[:], dtb_d[l].partition_broadcast(128))
            s.dma("sp", scr[0:16, 0:256], w2_d[l])
            cp("pool", w2_sb[:], scr[0:16, 0:256])
            s.dma("sp", gb_sb[:], gb_d[l:l + 1, :])
            s.dma("sp", dng[:], dng_d[l].rearrange("(d o) -> d o", o=1))
            s.dma("sp", glg[:], glg_d[l].rearrange("(d o) -> d o", o=1))
            s.dma("sp", w8st[:, :, 0:8], w_in_l[:, :, 2048:2056])
            s.dma("sp", w8st[:, :, 8:24], w_in_l[:, :, 3592:3608])
            cp("pool", w8bf[:], w8st[:])
            act(nA_b[:], alog_b[:], AF.Exp)
            ts("dve", nA_b[:], nA_b[:], -1.0, ALU.mult)

            for t in range(NT):
                for half in range(2):
                    p = pj[(2 * t + half) % 2]
                    for j in range(4):
                        k = half * 4 + j
                        tr(p[:, j * 128:(j + 1) * 128], x_tm[:, t, k * 128:(k + 1) * 128], ident[:])
                    eng = "act" if half == 0 else "dve"
                    cp(eng, xT[:, half * 4:half * 4 + 4, t * 128:(t + 1) * 128],
                       p[:, :].rearrange("p (a b) -> p a b", a=4))
            dbg("xT", xT[:, 0, :])

            bgp = pb
            for t in range(NT):
                for k in range(KC):
                    mm(bgp[:, t * 8:(t + 1) * 8], xT[:, k, t * 128:(t + 1) * 128], w8bf[:, k, 0:8],
                       start=(k == 0), stop=(k == KC - 1))
            bg3 = bgp[:, 0:128].rearrange("p (t c) -> p t c", c=8)
            dbg("b0", xT[:, 0, 0:128])
            act(beta[:], bg3[:, :, 0:4], AF.Sigmoid)
            dbg("b1", beta[:])
            tt("dve", tmp64[:], bg3[:, :, 4:8], dtb_b[:].unsqueeze(1).broadcast_to([128, NT, 4]), ALU.add)
            dbg("b2", tmp64[:])
            act(tmp64[:], tmp64[:], AF.Exp)
            act(tmp64[:], tmp64[:], AF.Ln, bias=1.0)
            dbg("b3", tmp64[:])
            tt("dve", gsb[:], tmp64[:], nA_b[:].unsqueeze(1).broadcast_to([128, NT, 4]), ALU.mult)
            g2 = gsb[:].rearrange("p t h -> p (t h)")
            dbg("b4", gsb[:])
            mm(pa[:, 256:320], maskLE[:], g2)
            mm(pa[:, 320:384], ones_f[:], g2)
            Gp = pa[:, 256:320].rearrange("p (t h) -> p t h", h=4)
            Glp = pa[:, 320:384].rearrange("p (t h) -> p t h", h=4)
            dbg("b5", gsb[:])
            cp("dve", G_sb[:], Gp)
            dbg("b6", gsb[:])
            act(eG[:], G_sb[:], AF.Exp)
            dbg("b7", gsb[:])
            cp("dve", eGl[:], Glp)
            act(eGl[:], eGl[:], AF.Exp)
            dbg("b8", gsb[:])
            tt("dve", tmp64[:], Glp, G_sb[:], ALU.subtract)
            dbg("b9", gsb[:])
            act(ekd[:], tmp64[:], AF.Exp)
            dbg("b10", gsb[:])
            tt("dve", bG[:], beta[:], eG[:], ALU.mult)
            dbg("beta", beta[:])
            dbg("gsb", gsb[:])
            dbg("G_sb", G_sb[:])
            for tb in range(4):
                p = pj[tb % 2]
                for k in range(KC):
                    mm(p[0:16, :], w8bf[:, k, 8:24], xT[:, k, tb * 512:(tb + 1) * 512], start=(k == 0), stop=(k == KC - 1))
                cp("act", grT[:, tb * 512:(tb + 1) * 512], p[0:16, :])
            dbg("grT", grT[:])

            def b4(ap3):
                return ap3.broadcast_to([128, GT, 128])
            v3 = lambda p_, g=GT: p_.rearrange("p (g c) -> p g c", g=g)
            id4 = ident[:].unsqueeze(1).broadcast_to([128, GT, 128])
            sil = [L1[:].rearrange("p g c -> p (g c)"), L2[:].rearrange("p g c -> p (g c)")]
            HG = GT // 2

            def rec_a(h, t, g, kd, at):
                r = t % NS
                if t == 0:
                    cp("dve", vnew[r][:], u_sb[:, g, :])
                else:
                    mm(pr[:, 0:128], wT_sb[:, g, :], S_bf[:])
                    tt("dve", vnew[r][:], u_sb[:, g, :], pr[:, 0:128], ALU.subtract)

            def rec_b(h, t, g, kd, at):
                r = t % NS
                tsl = slice(t * 128, (t + 1) * 128)
                oslot = po[:, g * 128:(g + 1) * 128]
                if t == 0:
                    mm(oslot, vnew[r][:], at[:, g, :])
                else:
                    mm(oslot, S_bf[:], qk[:, 0, tsl], start=True, stop=False)
                    mm(oslot, vnew[r][:], at[:, g, :], start=False, stop=True)
                mm(pr[:, 128:256], kd[:, g, :], vnew[r][:])
                if t == 0:
                    cp("dve", S32[:], pr[:, 128:256])
                else:
                    stt(S32[:], S32[:], eGl[:, t, h:h + 1], pr[:, 128:256], ALU.mult, ALU.add)
                cp("act", S_bf[:], S32[:])
                if g == GT - 1:
                    cp("act", raw[:, 3 + (t - GT + 1) * 128:3 + (t + 1) * 128], po[:, :])

            for h in range(4):
                qh = qk[:, 0, :]
                kh = qk[:, 1, :]
                def part_a(hh, ci):
                    c0 = ci * 512 + hh * 128
                    wb = load_w(w_in_l[:, :, c0:c0 + 128], 128)
                    w4 = convw[:, ci * 4 + hh, :]
                    for j in range(4):
                        ts("pool", dgw[:, j, :], ident[:], w4[:, j:j + 1], ALU.mult, 1.0, ALU.mult)

                    def ev(tb, p):
                        cp("dve", rawr[:, 3 + tb * 512:3 + (tb + 1) * 512], p)
                    return proj_steps(wb, 128, ev)

                def conv_b(ci, dst):
                    for tb in range(4):
                        sl = slice(tb * 512, (tb + 1) * 512)
                        p = pr if tb % 2 == 0 else po
                        for j in range(4):
                            mm(p[:, :], dgw[:, j, :], rawr[:, j + tb * 512:j + tb * 512 + 512], start=(j == 0), stop=(j == 3))
                        act(dst[:, sl] if ci == 2 else acc[:, sl], p[:, :], AF.Silu)

                def l2n(ci, dst):
                    def outf(tb, rkb):
                        sl = slice(tb * 512, (tb + 1) * 512)
                        if ci == 0:
                            stt(dst[:, sl], acc[:, sl], 128.0 ** -0.5, rkb, ALU.mult, ALU.mult)
                        else:
                            tt("dve", dst[:, sl], acc[:, sl], rkb, ALU.mult)
                    return rms_steps(lambda tb: acc[:, tb * 512:(tb + 1) * 512], 1.0, outf, sil)

                def proj_z():
                    wb = load_w(w_in_l[:, :, 1536 + h * 128:1536 + (h + 1) * 128], 128)

                    def evz(tb, p):
                        act(zs[:, tb * 512:(tb + 1) * 512], p, AF.Silu)
                    proj_fm(wb, 128, evz)

                if h == 0:
                    interleave(part_a(h, 0))
                conv_b(0, qh)
                interleave(part_a(h, 1), l2n(0, qh))
                conv_b(1, kh)
                interleave(part_a(h, 2), l2n(1, kh))
                conv_b(2, vT[:])
                proj_z()
                if h == 3:
                    cur["wob"] = outproj_w(0, acc[:].bitcast(BF16)[:, 0:4096])
                if h == 0:
                    dbg("qhat", qh)
                    dbg("khat", kh)
                    dbg("vT", vT[:])
                    dbg("zs", zs[:])

                def ep_block(tb, rkb, h=h):
                    sl = slice(tb * 512, (tb + 1) * 512)
                    osl = raw[:, 3 + tb * 512:3 + (tb + 1) * 512]
                    act(sq[:], osl, AF.Square)
                    mm(pr[:, :], ones_bf[:], sq[:])
                    act(rkb, pr[:, :], AF.Ln, bias=EPS, scale=1.0 / 128.0)
                    act(rkb, rkb, AF.Exp, scale=-0.5)
                    tt("dve", rkb, rkb, osl, ALU.mult)
                    stt(oT_all[:, h, sl], rkb, dng[:, 0:1], zs[:, sl], ALU.mult, ALU.mult)

                pend = None
                for tg in range(NT // GT):
                    t0 = tg * GT
                    gsl = slice(t0 * 128, (t0 + GT) * 128)
                    kdec = kdec2[tg % 2]
                    attnT = attnT2[tg % 2]
                    gcol4 = b4(gsb[:, t0:t0 + GT, h:h + 1])
                    tt("pool", L2[:], maskGT[:].unsqueeze(1).broadcast_to([128, GT, 128]), gcol4, ALU.mult)
                    tt("pool", L1[:], maskLE[:].unsqueeze(1).broadcast_to([128, GT, 128]), gcol4, ALU.mult)
                    for g in range(GT):
                        c = slice(g * 128, (g + 1) * 128)
                        mm(pc[:, c], maskLE[:], L2[:, g, :], start=True, stop=False)
                        mm(pc[:, c], ident_bf[:], negSL[:], start=False, stop=True)
                    for g in range(GT):
                        tsl = slice((t0 + g) * 128, (t0 + g + 1) * 128)
                        mm(pa[:, g * 128:(g + 1) * 128], kh[:, tsl], kh[:, tsl])
                    for g in range(GT):
                        c = slice(g * 128, (g + 1) * 128)
                        mm(pj[0][:, c], L2[:, g, :], maskLE[:], start=True, stop=False)
                        mm(pj[0][:, c], ident_bf[:], negUI[:], start=False, stop=True)
                    for g in range(GT):
                        c = slice(g * 128, (g + 1) * 128)
                        mm(pj[1][:, c], ones_f[:], L1[:, g, :])
                    act(E_sb[:], v3(pc[:, :]), AF.Exp)
                    act(ET_sb[:], v3(pj[0][:, :]), AF.Exp)
                    act(eGB[:], v3(pj[1][:, :]), AF.Exp)
                    tt("dve", E_sb[:], E_sb[:], b4(beta[:, t0:t0 + GT, h:h + 1]), ALU.mult)
                    Ck = CB[0]
                    v4 = lambda p_: p_.rearrange("p (h g c) -> p h g c", h=2, g=HG)
                    CkC = lambda X, g: X[:, g // HG, 0, g % HG, :]
                    CkB = lambda X, g: X[:, g // HG, 1, g % HG, :]
                    tt("dve", Ck[:, :, 0], v4(pa[:, :]), v4(E_sb[:].rearrange("p g c -> p (g c)")), ALU.mult)
                    for g in range(GT):
                        tr(pa[:, g * 128:(g + 1) * 128], CkC(Ck, g), ident[:])
                    for g in range(GT):
                        tsl = slice((t0 + g) * 128, (t0 + g + 1) * 128)
                        c = slice(g * 128, (g + 1) * 128)
                        mm(pb[:, c], kh[:, tsl], qh[:, tsl])
                        tr(pt[:, c], kh[:, tsl], ident_bf[:])
                        tr(pt[:, 512 + g * 128:512 + (g + 1) * 128], vT[:, tsl], ident_bf[:])
                    cp("act", Ck[:, :, 1], v4(pa[:, :]))
                    tt("dve", attnT[:], v3(pb[:, :]), ET_sb[:], ALU.mult)
                    tt("dve", Pg[:], id4, v3(pa[:, :]), ALU.subtract)
                    tt("dve", kbg[:], v3(pt[:, 0:512]), b4(bG[:, t0:t0 + GT, h:h + 1]), ALU.mult)
                    tt("dve", kdec[:], v3(pt[:, 0:512]), b4(ekd[:, t0:t0 + GT, h:h + 1]), ALU.mult)
                    tt("dve", vb[:], v3(pt[:, 512:1024]), b4(beta[:, t0:t0 + GT, h:h + 1]), ALU.mult)
                    qg3 = qh[:, gsl].rearrange("p (g c) -> p g c", g=GT)
                    tt("pool", qg3, qg3, eGB[:], ALU.mult)
                    bankCB = (pb, pc)
                    bankP = (pj[0], pj[1])
                    for lev in range(1, 8):
                        Cn = CB[lev % 2]
                        for hf in range(2):
                            gs = range(hf * HG, (hf + 1) * HG)
                            pcb = bankCB[hf]
                            pp_ = bankP[hf]
                            if lev <= 6:
                                for gi, g in enumerate(gs):
                                    mm(pcb[:, gi * 128:(gi + 1) * 128], CkB(Ck, g), CkC(Ck, g))
                                if lev <= 5:
                                    for gi, g in enumerate(gs):
                                        mm(pcb[:, 256 + gi * 128:256 + (gi + 1) * 128], CkC(Ck, g), CkB(Ck, g))
                            if lev >= 2:
                                for gi, g in enumerate(gs):
                                    mm(pp_[:, gi * 128:(gi + 1) * 128], CkC(Ck, g), Pg[:, g, :])
                            if lev <= 5:
                                cp("act", Cn[:, hf], pcb[:, :].rearrange("p (a g c) -> p a g c", a=2, g=HG))
                            elif lev == 6:
                                cp("act", Cn[:, hf, 0], v3(pcb[:, 0:256], HG))
                            if lev >= 2:
                                tt("dve", Pg[:, hf * HG:(hf + 1) * HG, :], Pg[:, hf * HG:(hf + 1) * HG, :],
                                   v3(pp_[:, 0:256], HG), ALU.add)
                            if pend is not None and 1 <= lev <= GT:
                                (rec_a if hf == 0 else rec_b)(*pend[lev - 1])
                            if tg == NT // GT - 1 and hf == 1 and 5 <= lev <= 7:
                                ep_block(lev - 5, sil[lev % 2])
                        Ck = Cn
                    cp("act", TT_bf[:], Pg[:])
                    for g in range(GT):
                        c = slice(g * 128, (g + 1) * 128)
                        mm(pb[:, c], TT_bf[:, g, :], vb[:, g, :])
                        mm(pc[:, c], kbg[:, g, :], TT_bf[:, g, :])
                    cp("act", u_sb[:], v3(pb[:, :]))
                    cp("dve", wT_sb[:], v3(pc[:, :]))
                    pend = [(h, t0 + g, g, kdec, attnT) for g in range(GT)]
                    if tg == NT // GT - 1:
                        for args in pend:
                            rec_a(*args)
                            rec_b(*args)
                        pend = None
                    if h == 0 and tg == 0:
                        dbg("TT0", TT_bf[:, 0, :])
                def ep_last():
                    ep_block(3, sil[0])
                    yield
                if h < 3:
                    interleave(ep_last(), part_a(h + 1, 0))
                else:
                    interleave(ep_last())
                if h == 0:
                    dbg("o_raw0", raw[:, 3:3 + T])
                    dbg("oT0", oT_all[:, 0, :])

            dbg("gdn_done", gsb[:])
            outproj(0, True)
            dbg("op0", gsb[:])

            lf = acc[:, 0:NT * 64].rearrange("p (t d) -> p t d", d=64)
            elast = tmp64[0:64].rearrange("p t h -> p (t h)")[:, 0:NT]
            vtm = vT[:].rearrange("p (t d) -> p t d", d=128)
            oTg = raw[:, 3:3 + T]
            for h in range(4):
                qh = qk[0:64, 0, :]
                kh = qk[0:64, 1, :]
                def gate_lf(hh):
                    for half in range(2):
                        p = pj[half]
                        for j in range(8):
                            t = half * 8 + j
                            mm(p[:, j * 64:(j + 1) * 64], grT[:, t * 128:(t + 1) * 128], w2_sb[:, hh * 64:(hh + 1) * 64],
                               start=True, stop=False)
                            mm(p[:, j * 64:(j + 1) * 64], ones_f[0:1, :], gb_sb[0:1, hh * 64:(hh + 1) * 64], start=False, stop=True)
                        act(acc[:, half * 512:(half + 1) * 512], p[:, :], AF.Exp, scale=-1.0)
                    act(acc[:, 0:NT * 64], acc[:, 0:NT * 64], AF.Ln, bias=1.0)

                if h == 0:
                    gate_lf(0)
                    dbg("lf", acc[:, 0:NT * 64])
                def qk_steps(hh):
                    wq = load_w(w_in_l[:, :, 2056 + hh * 64:2056 + (hh + 1) * 64], 64)
                    wk = load_w(w_in_l[:, :, 2312 + hh * 64:2312 + (hh + 1) * 64], 64)

                    def gen():
                        for tb in range(4):
                            sl = slice(tb * 512, (tb + 1) * 512)
                            for j in range(4):
                                t = tb * 4 + j
                                mm(pc[0:64, j * 128:(j + 1) * 128], lf[:, t, :], maskLE[:])
                            act(ebuf[0], pc[0:64, :], AF.Exp, scale=-1.0 / 16.0)
                            act(ebuf[1], pc[0:64, :], AF.Exp, scale=1.0 / 16.0)
                            p = pj[0]
                            for k in range(KC):
                                mm(p[0:64, :], wq[:, k, 0:64], xT[:, k, sl], start=(k == 0), stop=(k == KC - 1))
                            stt(qh[:, sl], p[0:64, :], 0.125, ebuf[0], ALU.mult, ALU.mult)
                            p = pj[1]
                            for k in range(KC):
                                mm(p[0:64, :], wk[:, k, 0:64], xT[:, k, sl], start=(k == 0), stop=(k == KC - 1))
                            tt("dve", kh[:, sl], p[0:64, :], ebuf[1], ALU.mult)
                            for j in range(4):
                                t = tb * 4 + j
                                cp("dve", elast[:, t:t + 1], ebuf[0][:, j * 128 + 127:j * 128 + 128])
                            yield
                    return gen()

                if h == 0:
                    interleave(qk_steps(0))
                wz = load_w(w_in_l[:, :, 3080 + h * 128:3080 + (h + 1) * 128], 128)

                def evz2(tb, p):
                    act(zs[:, tb * 512:(tb + 1) * 512], p, AF.Silu)
                proj_fm(wz, 128, evz2)
                wv = load_w(w_in_l[:, :, 2568 + h * 128:2568 + (h + 1) * 128], 128)
                for t in range(NT):
                    p = pj[t % 2]
                    for k in range(KC):
                        mm(p[:, 0:128], xT[:, k, t * 128:(t + 1) * 128], wv[:, k, :], start=(k == 0), stop=(k == KC - 1))
                    cp("act" if t % 2 == 0 else "dve", vtm[:, t, :], p[:, 0:128])
                if h == 0:
                    dbg("gq", qh)
                    dbg("gk", kh)
                    dbg("gv", vT[:])
                if h == 3:
                    cur["wob"] = outproj_w(1, acc[:].bitcast(BF16)[:, 0:4096])
                m01b = m01UI[:].unsqueeze(1).broadcast_to([128, 4, 128])
                for tg in range(NT // 4):
                    t0 = tg * 4
                    kt4 = kbg if tg % 2 == 0 else vb
                    at4 = attnT2[tg % 2]
                    for g in range(4):
                        tsl = slice((t0 + g) * 128, (t0 + g + 1) * 128)
                        tr(pt[:, g * 64:(g + 1) * 64], kh[:, tsl], ident_bf[0:64, 0:64])
                        mm(pa[:, g * 128:(g + 1) * 128], kh[:, tsl], qh[:, tsl])
                    cp("act", kt4[:, :, 0:64], pt[:, 0:256].rearrange("p (g c) -> p g c", g=4))
                    tt("dve", at4[:], pa[:, :].rearrange("p (g c) -> p g c", g=4), m01b, ALU.mult)
                    for g in range(4):
                        mm(pb[0:64, g * 128:(g + 1) * 128], kt4[:, g, 0:64], vtm[:, t0 + g, :])
                    for g in range(4):
                        t = t0 + g
                        tsl = slice(t * 128, (t + 1) * 128)
                        Rn = Pg[0:64, t % 4, :]
                        Rp = Pg[0:64, (t - 1) % 4, :]
                        Sn = wT_sb[0:64, t % 4, :]
                        Sp = wT_sb[0:64, (t - 1) % 4, :]
                        oslot = po[:, g * 128:(g + 1) * 128]
                        if t == 0:
                            mm(oslot, vtm[:, t, :], at4[:, g, :])
                            cp("dve", Rn, pb[0:64, 0:128])
                        else:
                            mm(oslot, Sp, qh[:, tsl], start=True, stop=False)
                            mm(oslot, vtm[:, t, :], at4[:, g, :], start=False, stop=True)
                            stt(Rn, Rp, elast[:, t - 1:t], pb[0:64, g * 128:(g + 1) * 128], ALU.mult, ALU.add)
                        act(Sn, Rn, AF.Copy, scale=elast[:, t:t + 1])
                    cp("act", oTg[:, t0 * 128:(t0 + 4) * 128], po[:, :])
                if h < 3:
                    gate_lf(h + 1)
                def outf_g(tb, rkb, h=h):
                    sl = slice(tb * 512, (tb + 1) * 512)
                    tt("dve", rkb, rkb, oTg[:, sl], ALU.mult)
                    stt(oT_all[:, h, sl], rkb, glg[:, 0:1], zs[:, sl], ALU.mult, ALU.mult)
                epg = rms_steps(lambda tb: oTg[:, tb * 512:(tb + 1) * 512], 1.0 / 128.0, outf_g,
                                (rk, Pg[:].rearrange("p g c -> p (g c)")))
                if h < 3:
                    interleave(qk_steps(h + 1), epg)
                else:
                    interleave(epg)
                if h == 0:
                    dbg("o_raw4", oTg)
                    dbg("oT4", oT_all[:, 0, :])
            lng_t = CB[0][:].rearrange("p a b g c -> p (a b g c)")
            lnb_t = CB[1][:].rearrange("p a b g c -> p (a b g c)")
            s.dma("sp", lng_t, lng_d[l].partition_broadcast(128))
            s.dma("sp", lnb_t, lnb_d[l].partition_broadcast(128))
            outproj(1, False, ln=(lng_t, lnb_t, last))
            dbg("xout", x_tm[:, 0, :])
          except _Stop:
            s.dma("sp", y_d[0:128, :], x_tm[:, 0, :])
            break
        s.emit()
    return nc, dbg_out


_CACHE = {}


def _prep_inputs(inputs, b):
    f = lambda a: np.ascontiguousarray(np.asarray(a, dtype=np.float32))
    m = {
        "x": f(inputs["x"][b]),
        "w_in": f(inputs["w_in"]),
        "conv_wt": f(np.transpose(np.asarray(inputs["conv_w"]), (0, 2, 1))),
        "dn_a_log": f(inputs["dn_a_log"]),
        "dn_dt_bias": f(inputs["dn_dt_bias"]),
        "gla_gate_w2": f(inputs["gla_gate_w2"]),
        "gla_gate_b": f(inputs["gla_gate_b"]),
        "dn_norm_g": f(inputs["dn_norm_g"]),
        "gla_norm_g": f(inputs["gla_norm_g"]),
        "w_out": f(inputs["w_out"]),
        "ln_g": f(inputs["ln_g"]),
        "ln_b": f(inputs["ln_b"]),
    }
    return m


FUSED = True


def kernel(**inputs):
    in_maps = [_prep_inputs(inputs, b) for b in range(8)]
    groups = [list(range(DEPTH))] if FUSED else [[l] for l in range(DEPTH)]
    for grp in groups:
        key = tuple(grp)
        if key not in _CACHE:
            _CACHE[key] = build(grp)[0]
        res = run_bass_kernel_spmd(_CACHE[key], in_maps, core_ids=list(range(8)))
        ys = [np.asarray(r["y"]) for r in res.results]
        for b in range(8):
            in_maps[b]["x"] = np.ascontiguousarray(ys[b], dtype=np.float32)
    return np.stack(ys, axis=0).astype(np.float32)
```

```python
import contextlib
import numpy as np
import concourse.bass as bass
import concourse.mybir as mybir
from concourse.bass_utils import run_bass_kernel_spmd

F32 = mybir.dt.float32
BF16 = mybir.dt.bfloat16
ALU = mybir.AluOpType
AF = mybir.ActivationFunctionType

T = 2048
D = 1024
NT = 16
KC = 8
DEPTH = 2
D_IN = 3608
ALPHA = (2.0 * DEPTH) ** 0.25
EPS = 1e-6
NEG = -30000.0

NDMA = 12


def _region(ap):
    sp = str(ap.space).upper()
    if "DRAM" in sp:
        return None
    if "PSUM" in sp:
        return (ap.tensor.name, 0, 128, 0, 1 << 30)
    pat = ap.ap
    pstride, pn = pat[0]
    off = int(ap.offset)
    if pstride == 0:
        p_lo, f_lo, pn = 0, off, 1
    else:
        p_lo, f_lo = off // pstride, off % pstride
    ext = 1
    for st, cnt in pat[1:]:
        ext += abs(st) * (cnt - 1)
    esz = mybir.dt.size(ap.dtype)
    return (ap.tensor.name, p_lo, p_lo + pn, f_lo * esz, (f_lo + ext) * esz)


class Sched:
    ENG = ("pe", "dve", "act", "pool", "sp")

    def __init__(self, nc):
        self.nc = nc
        self.prog = {e: [] for e in self.ENG}
        self.cnt = {e: 0 for e in self.ENG}
        self.seen = {e: {} for e in self.ENG}
        self.dcnt = [0] * NDMA
        self.dnext = 0
        self.recs = {}
        self.out_dmas = []

    def _deps(self, reads, writes):
        deps = {}
        for lst, only_w in ((reads, True), (writes, False)):
            for ap in lst:
                r = _region(ap)
                if r is None:
                    continue
                for rec in self.recs.get(r[0], ()):
                    if only_w and not rec[6]:
                        continue
                    if rec[0] < r[2] and r[1] < rec[1] and rec[2] < r[4] and r[3] < rec[3]:
                        if deps.get(rec[4], 0) < rec[5]:
                            deps[rec[4]] = rec[5]
        return deps

    def _record(self, reads, writes, src, val):
        for ap in writes:
            r = _region(ap)
            if r is None:
                continue
            lst = self.recs.setdefault(r[0], [])
            lst[:] = [x for x in lst if not (r[1] <= x[0] and x[1] <= r[2] and r[3] <= x[2] and x[3] <= r[4])]
            lst.append([r[1], r[2], r[3], r[4], src, val, True])
        for ap in reads:
            r = _region(ap)
            if r is None:
                continue
            lst = self.recs.setdefault(r[0], [])
            for x in lst:
                if (not x[6]) and x[4] == src and x[0] == r[1] and x[1] == r[2] and x[2] == r[3] and x[3] == r[4]:
                    x[5] = max(x[5], val)
                    break
            else:
                lst.append([r[1], r[2], r[3], r[4], src, val, False])

    def _waits(self, eng, deps):
        out = []
        seen = self.seen[eng]
        for src, val in deps.items():
            if src == eng and eng == "pe":
                continue
            if seen.get(src, 0) >= val:
                continue
            seen[src] = val
            out.append((src, val))
        return out

    @staticmethod
    def _excl(reads, writes):
        ps = [a for a in reads if "PSUM" in str(a.space).upper()]
        if ps:
            reads = [a for a in reads if "PSUM" not in str(a.space).upper()]
            writes = list(writes) + ps
        return reads, writes

    def op(self, eng, fn, reads=(), writes=()):
        reads, writes = self._excl(list(reads), list(writes))
        deps = self._deps(reads, writes)
        idx = self.cnt[eng] + 1
        self.cnt[eng] = idx
        waits = self._waits(eng, deps)
        self.prog[eng].append((waits, fn, (eng, 1)))
        self._record(reads, writes, eng, idx)
        return idx

    def dma(self, queue, out, in_, **kw):
        deps = self._deps([in_], [out])
        c = self.dnext
        self.dnext = (self.dnext + 1) % NDMA
        src = "dma%d" % c
        if self.dcnt[c] > 0 and deps.get(src, 0) < 16 * self.dcnt[c]:
            deps[src] = 16 * self.dcnt[c]
        self.dcnt[c] += 1
        val = 16 * self.dcnt[c]
        waits = self._waits(queue, deps)

        def fn(e, out=out, in_=in_, kw=kw):
            return e.dma_start(out=out, in_=in_, **kw)

        self.prog[queue].append((waits, fn, (src, 16)))
        self._record([in_], [out], src, val)
        if _region(out) is None:
            self.out_dmas.append((src, val))

    def emit(self):
        nc = self.nc
        with contextlib.ExitStack() as st:
            sems = {}
            for e in self.ENG:
                sems[e] = st.enter_context(nc.semaphore("s_" + e))
            for c in range(NDMA):
                sems["dma%d" % c] = st.enter_context(nc.semaphore("s_dma%d" % c))
            fin = {}
            for src, val in self.out_dmas:
                fin[src] = max(fin.get(src, 0), val)
            block = st.enter_context(nc.Block())
            handles = {"pe": "tensor", "dve": "vector", "act": "scalar", "pool": "gpsimd", "sp": "sync"}

            def make(ename):
                prog = self.prog[ename]

                def body(e):
                    for waits, fn, (isrc, inc) in prog:
                        for src, val in waits:
                            e.wait_ge(sems[src], val)
                        fn(e).then_inc(sems[isrc], inc)
                    if ename == "sp":
                        for src, val in fin.items():
                            e.wait_ge(sems[src], val)
                return body

            for ename in self.ENG:
                getattr(block, handles[ename])(make(ename))


class _Stop(Exception):
    pass


def build(layers, debug=(), stop=None):
    nc = bass.Bass("TRN2", target_bir_lowering=False)
    dram = {}

    def din(name, shape):
        dram[name] = nc.dram_tensor(name, list(shape), F32, kind="ExternalInput").ap()
        return dram[name]

    x_d = din("x", [T, D])
    w_in_d = din("w_in", [DEPTH, D, D_IN])
    conv_d = din("conv_wt", [DEPTH, 1536, 4])
    alog_d = din("dn_a_log", [DEPTH, 4])
    dtb_d = din("dn_dt_bias", [DEPTH, 4])
    w2_d = din("gla_gate_w2", [DEPTH, 16, 256])
    gb_d = din("gla_gate_b", [DEPTH, 256])
    dng_d = din("dn_norm_g", [DEPTH, 128])
    glg_d = din("gla_norm_g", [DEPTH, 128])
    w_out_d = din("w_out", [DEPTH, D, D])
    lng_d = din("ln_g", [DEPTH, D])
    lnb_d = din("ln_b", [DEPTH, D])
    y_d = nc.dram_tensor("y", [T, D], F32, kind="ExternalOutput").ap()
    dbg_out = {}

    with contextlib.ExitStack() as st:
        def sb(name, shape, dt=F32):
            return st.enter_context(nc.sbuf_tensor(name, list(shape), dt))

        def ps(name, shape, dt=F32):
            return st.enter_context(nc.psum_tensor(name, list(shape), dt))

        s = Sched(nc)

        x_tm = sb("x_tm", [128, NT, D])
        xT = sb("xT", [128, KC, T], BF16)
        oT_all = sb("oT_all", [128, 4, T], BF16)
        wst = [sb("wst%d" % i, [128, KC, 128]) for i in range(2)]
        wbf = [sb("wbf%d" % i, [128, KC, 128], BF16) for i in range(2)]
        raw = sb("raw", [128, 3 + T])
        acc = sb("acc", [128, T])
        qk = sb("qk", [128, 2, T], BF16)
        vT = sb("vT", [128, T], BF16)
        zs = sb("zs", [128, T], BF16)
        ident = sb("ident", [128, 128])
        ident_bf = sb("ident_bf", [128, 128], BF16)
        maskLE = sb("maskLE", [128, 128])
        maskGT = sb("maskGT", [128, 128])
        ones_f = sb("ones_f", [128, 128])
        ones_bf = sb("ones_bf", [128, 128], BF16)
        negSL = sb("negSL", [128, 128], BF16)
        negUI = sb("negUI", [128, 128], BF16)
        m01UI = sb("m01UI", [128, 128])
        convw = sb("convw", [128, 12, 4])
        alog_b = sb("alog_b", [128, 4])
        dtb_b = sb("dtb_b", [128, 4])
        nA_b = sb("nA_b", [128, 4])
        w2_sb = sb("w2_sb", [16, 256], BF16)
        gb_sb = sb("gb_sb", [1, 256])
        dng = sb("dng", [128, 1])
        glg = sb("glg", [128, 1])
        w8st = sb("w8st", [128, KC, 24])
        w8bf = sb("w8bf", [128, KC, 24], BF16)
        grT = sb("grT", [16, T], BF16)
        beta = sb("beta", [128, NT, 4])
        gsb = sb("gsb", [128, NT, 4])
        G_sb = sb("G_sb", [128, NT, 4])
        eG = sb("eG", [128, NT, 4])
        bG = sb("bG", [128, NT, 4])
        eGl = sb("eGl", [128, NT, 4])
        ekd = sb("ekd", [128, NT, 4])
        tmp64 = sb("tmp64", [128, NT, 4])
        NS = 2
        GT = 4
        L2 = sb("L2g", [128, GT, 128])
        L1 = sb("L1g", [128, GT, 128])
        E_sb = L2
        eGB = L1
        Pg = sb("Pg", [128, GT, 128])
        ET_sb = Pg
        CB = [sb("CB%d" % j, [128, 2, 2, GT // 2, 128]) for j in range(2)]
        TT_bf = sb("TTg", [128, GT, 128], BF16)
        kbg = sb("kbg", [128, GT, 128], BF16)
        kdec2 = [sb("kdec%d" % i, [128, GT, 128], BF16) for i in range(2)]
        vb = sb("vbg", [128, GT, 128], BF16)
        attnT2 = [sb("attnTg%d" % i, [128, GT, 128], BF16) for i in range(2)]
        wT_sb = sb("wTg", [128, GT, 128], BF16)
        u_sb = sb("ug", [128, GT, 128], BF16)
        vnew = [sb("vnew_%d" % i, [128, 128], BF16) for i in range(NS)]
        sq2 = sb("sq2", [128, 512], BF16)
        S32 = sb("S32", [128, 128])
        S_bf = sb("S_bf", [128, 128], BF16)
        sq = sb("sq", [128, 512], BF16)
        scr = sb("scr", [128, 512])
        rk = scr[:, 0:512]
        ebuf = [L1[0:64].rearrange("p g c -> p (g c)"), L2[0:64].rearrange("p g c -> p (g c)")]
        rawr = sb("rawr", [128, 4 + T], mybir.dt.float32r)
        dgw = sb("dgw", [128, 4, 128], mybir.dt.float32r)
        stat = sb("stat", [128, 64])

        pj = [ps("pj%d" % i, [128, 512]) for i in range(2)]
        pa = ps("pa", [128, 512])
        pb = ps("pb", [128, 512])
        pt = ps("pt", [128, 1024], BF16)
        pc = ps("pc", [128, 512])
        pr = ps("pr", [128, 512])
        po = ps("po", [128, 512])

        def mm(out, lhsT, rhs, start=True, stop=True):
            rd = [lhsT, rhs] + ([] if start else [out])
            s.op("pe", lambda e: e.matmul(out, lhsT=lhsT, rhs=rhs, start=start, stop=stop), reads=rd, writes=[out])

        def tr(out, in_, idn):
            s.op("pe", lambda e: e.transpose(out, in_, idn), reads=[in_, idn], writes=[out])

        def act(out, in_, func, bias=None, scale=None, accum=None):
            kw = {}
            rd = [in_]
            if bias is not None:
                kw["bias"] = bias
                if not isinstance(bias, float):
                    rd.append(bias)
            if scale is not None:
                kw["scale"] = scale
                if not isinstance(scale, float):
                    rd.append(scale)
            wr = [out]
            if accum is not None:
                kw["accum_out"] = accum
                wr.append(accum)
            s.op("act", lambda e: e.activation(out=out, in_=in_, func=func, **kw), reads=rd, writes=wr)

        def tt(eng, out, in0, in1, op):
            s.op(eng, lambda e: e.tensor_tensor(out=out, in0=in0, in1=in1, op=op), reads=[in0, in1], writes=[out])

        def ts(eng, out, in0, s1, op0, s2=None, op1=None):
            rd = [in0] + [x for x in (s1, s2) if x is not None and not isinstance(x, float)]
            if op1 is None:
                s.op(eng, lambda e: e.tensor_scalar(out=out, in0=in0, scalar1=s1, scalar2=None, op0=op0), reads=rd, writes=[out])
            else:
                s.op(eng, lambda e: e.tensor_scalar(out=out, in0=in0, scalar1=s1, scalar2=s2, op0=op0, op1=op1), reads=rd, writes=[out])

        def stt(out, in0, sc, in1, op0, op1):
            rd = [in0, in1] + ([] if isinstance(sc, float) else [sc])
            s.op("dve", lambda e: e.scalar_tensor_tensor(out=out, in0=in0, scalar=sc, in1=in1, op0=op0, op1=op1), reads=rd, writes=[out])

        def cp(eng, out, in_):
            if eng == "act":
                s.op("act", lambda e: e.copy(out=out, in_=in_), reads=[in_], writes=[out])
            else:
                s.op(eng, lambda e: e.tensor_copy(out=out, in_=in_), reads=[in_], writes=[out])

        def memset(ap, v):
            s.op("pool", lambda e: e.memset(ap, v), writes=[ap])

        def asel(out, in_, base, cm, step, cmp, fill):
            s.op("pool", lambda e: e.affine_select(out=out, in_=in_, pattern=[[step, 128]], base=base,
                                                   channel_multiplier=cm, compare_op=cmp, fill=fill),
                 reads=[in_], writes=[out])

        def dbg(name, ap):
            if stop == name:
                raise _Stop()
            if name in debug:
                shp = list(ap.shape)
                d = nc.dram_tensor("dbg_" + name, shp, ap.dtype, kind="ExternalOutput").ap()
                dbg_out[name] = d
                s.dma("sp", d, ap)

        wstate = {"st": 0, "bf": 0}

        def load_w(src_ap, ncols):
            a = wst[wstate["st"] % 2]
            b = wbf[wstate["bf"] % 2]
            wstate["st"] += 1
            wstate["bf"] += 1
            s.dma("sp", a[:, :, 0:ncols], src_ap)
            cp("pool", b[:, :, 0:ncols], a[:, :, 0:ncols])
            return b

        def proj_fm(wb, M, evac):
            for tb in range(4):
                p = pj[tb % 2]
                for k in range(KC):
                    mm(p[0:M, :], wb[:, k, 0:M], xT[:, k, tb * 512:(tb + 1) * 512], start=(k == 0), stop=(k == KC - 1))
                evac(tb, p[0:M, :])

        def ln_stats(t0, n):
            st_ = stat[:, ((t0 // n) % 2) * 32:((t0 // n) % 2) * 32 + 32].rearrange("p (a j) -> p a j", j=4)
            zo = raw[:, 3:3 + D]
            for j in range(n):
                z = x_tm[:, t0 + j, :]
                act(zo, z, AF.Identity, accum=st_[:, 0, j:j + 1])
                act(zo, z, AF.Square, accum=st_[:, 1, j:j + 1])
            ts("dve", st_[:, 2, 0:n], st_[:, 0, 0:n], 1.0 / D, ALU.mult)
            tt("dve", st_[:, 3, 0:n], st_[:, 2, 0:n], st_[:, 2, 0:n], ALU.mult)
            stt(st_[:, 4, 0:n], st_[:, 1, 0:n], 1.0 / D, st_[:, 3, 0:n], ALU.mult, ALU.subtract)
            act(st_[:, 5, 0:n], st_[:, 4, 0:n], AF.Ln, bias=EPS)
            act(st_[:, 5, 0:n], st_[:, 5, 0:n], AF.Exp, scale=-0.5)
            stt(st_[:, 6, 0:n], st_[:, 2, 0:n], -1.0, st_[:, 5, 0:n], ALU.mult, ALU.mult)

        def ln_apply(t0, n, lng_t, lnb_t, last):
            st_ = stat[:, ((t0 // n) % 2) * 32:((t0 // n) % 2) * 32 + 32].rearrange("p (a j) -> p a j", j=4)
            for j in range(n):
                z = x_tm[:, t0 + j, :]
                act(z, z, AF.Identity, bias=st_[:, 6, j:j + 1], scale=st_[:, 5, j:j + 1])
                tt("dve", z, z, lng_t, ALU.mult)
                tt("pool", z, z, lnb_t, ALU.add)
                if last:
                    s.dma("sp", y_d[(t0 + j) * 128:(t0 + j + 1) * 128, :], z)

        def proj_steps(wb, M, evac):
            for tb in range(4):
                p = pj[tb % 2]
                for k in range(KC):
                    mm(p[0:M, :], wb[:, k, 0:M], xT[:, k, tb * 512:(tb + 1) * 512], start=(k == 0), stop=(k == KC - 1))
                evac(tb, p[0:M, :])
                yield

        def interleave(*gens):
            gens = list(gens)
            while gens:
                for g_ in list(gens):
                    try:
                        next(g_)
                    except StopIteration:
                        gens.remove(g_)

        def rms_steps(src, ln_scale, out_fn, rkbufs):
            sqb = (sq[:], sq2[:])
            pbk = (pa, pb)

            def stage1(tb):
                act(sqb[tb % 2], src(tb), AF.Square)
                mm(pbk[tb % 2][:, :], ones_bf[:], sqb[tb % 2])

            def stage2(tb):
                rkb = rkbufs[tb % 2]
                act(rkb, pbk[tb % 2][:, :], AF.Ln, bias=EPS, scale=ln_scale)
                act(rkb, rkb, AF.Exp, scale=-0.5)
                out_fn(tb, rkb)
            stage1(0)
            stage1(1)
            yield
            for tb in range(4):
                stage2(tb)
                if tb + 2 < 4:
                    stage1(tb + 2)
                yield

        def rms_blocks(src, ln_scale, out_fn, rkbufs):
            for _ in rms_steps(src, ln_scale, out_fn, rkbufs):
                pass

        def outproj_w(rh, buf2d):
            wob = buf2d.rearrange("p (k c) -> p k c", k=4)
            for cc in range(8):
                c0 = cc * 128
                a = wst[wstate["st"] % 2]
                wstate["st"] += 1
                s.dma("sp", a[:, 0:4, :], cur["w_out_l"][:, rh * 4:(rh + 1) * 4, c0:c0 + 128])
                cp("pool", wob[:, :, c0:c0 + 128], a[:, 0:4, :])
            return wob

        def outproj(rh, first, ln=None):
            wob = cur.get("wob")
            if wob is None:
                wob = outproj_w(rh, qk[:].rearrange("p a t -> p (a t)"))
            cur["wob"] = None
            for t in range(NT):
                for half in range(2):
                    p = pj[half]
                    for e4 in range(4):
                        mm(p[:, :], oT_all[:, e4, t * 128:(t + 1) * 128], wob[:, e4, half * 512:(half + 1) * 512],
                           start=(e4 == 0), stop=(e4 == 3))
                    xs = x_tm[:, t, half * 512:(half + 1) * 512]
                    if first:
                        stt(xs, xs, ALPHA, p[:, :], ALU.mult, ALU.add)
                    else:
                        tt("dve", xs, xs, p[:, :], ALU.add)
                if ln is not None and t % 4 == 3:
                    ln_stats(t - 3, 4)
                    if t >= 7:
                        ln_apply(t - 7, 4, *ln)
            if ln is not None:
                ln_apply(NT - 4, 4, *ln)

        cur = {}
        memset(ones_f[:], 1.0)
        memset(ones_bf[:], 1.0)
        asel(ident[:], ones_f[:], 0, -1, 1, ALU.is_equal, 0.0)
        cp("pool", ident_bf[:], ident[:])
        asel(maskLE[:], ones_f[:], 0, -1, 1, ALU.is_ge, 0.0)
        asel(maskGT[:], ones_f[:], 0, 1, -1, ALU.is_gt, 0.0)
        asel(m01UI[:], ones_f[:], 0, -1, 1, ALU.is_ge, 0.0)
        memset(L1[:, 0, :], 0.0)
        asel(L2[:, 0, :], L1[:, 0, :], 0, 1, -1, ALU.is_gt, NEG)
        cp("pool", negSL[:], L2[:, 0, :])
        asel(L2[:, 1, :], L1[:, 0, :], 0, -1, 1, ALU.is_ge, NEG)
        cp("pool", negUI[:], L2[:, 1, :])
        act(rawr[:, 0:3], L1[:, 0, 0:3], AF.Copy)

        for t in range(NT):
            s.dma("sp", x_tm[:, t, :], x_d[t * 128:(t + 1) * 128, :])

        for li, l in enumerate(layers):
          try:
            last = (li == len(layers) - 1)
            w_in_l = w_in_d[l].rearrange("(k p) c -> p k c", p=128)
            w_out_l = w_out_d[l].rearrange("(k p) c -> p k c", p=128)
            cur["w_out_l"] = w_out_l

            s.dma("sp", convw[:], conv_d[l].rearrange("(n p) j -> p n j", p=128))
            s.dma("sp", alog_b[:], alog_d[l].partition_broadcast(128))
            s.dma("sp", dtb_b[:], dtb_d[l].partition_broadcast(128))
            s.dma("sp", scr[0:16, 0:256], w2_d[l])
            cp("pool", w2_sb[:], scr[0:16, 0:256])
            s.dma("sp", gb_sb[:], gb_d[l:l + 1, :])
            s.dma("sp", dng[:], dng_d[l].rearrange("(d o) -> d o", o=1))
            s.dma("sp", glg[:], glg_d[l].rearrange("(d o) -> d o", o=1))
            s.dma("sp", w8st[:, :, 0:8], w_in_l[:, :, 2048:2056])
            s.dma("sp", w8st[:, :, 8:24], w_in_l[:, :, 3592:3608])
            cp("pool", w8bf[:], w8st[:])
            act(nA_b[:], alog_b[:], AF.Exp)
            ts("dve", nA_b[:], nA_b[:], -1.0, ALU.mult)

            for t in range(NT):
                for half in range(2):
                    p = pj[(2 * t + half) % 2]
                    for j in range(4):
                        k = half * 4 + j
                        tr(p[:, j * 128:(j + 1) * 128], x_tm[:, t, k * 128:(k + 1) * 128], ident[:])
                    eng = "act" if half == 0 else "dve"
                    cp(eng, xT[:, half * 4:half * 4 + 4, t * 128:(t + 1) * 128],
                       p[:, :].rearrange("p (a b) -> p a b", a=4))
            dbg("xT", xT[:, 0, :])

            bgp = pb
            for t in range(NT):
                for k in range(KC):
                    mm(bgp[:, t * 8:(t + 1) * 8], xT[:, k, t * 128:(t + 1) * 128], w8bf[:, k, 0:8],
                       start=(k == 0), stop=(k == KC - 1))
            bg3 = bgp[:, 0:128].rearrange("p (t c) -> p t c", c=8)
            dbg("b0", xT[:, 0, 0:128])
            act(beta[:], bg3[:, :, 0:4], AF.Sigmoid)
            dbg("b1", beta[:])
            tt("dve", tmp64[:], bg3[:, :, 4:8], dtb_b[:].unsqueeze(1).broadcast_to([128, NT, 4]), ALU.add)
            dbg("b2", tmp64[:])
            act(tmp64[:], tmp64[:], AF.Exp)
            act(tmp64[:], tmp64[:], AF.Ln, bias=1.0)
            dbg("b3", tmp64[:])
            tt("dve", gsb[:], tmp64[:], nA_b[:].unsqueeze(1).broadcast_to([128, NT, 4]), ALU.mult)
            g2 = gsb[:].rearrange("p t h -> p (t h)")
            dbg("b4", gsb[:])
            mm(pa[:, 256:320], maskLE[:], g2)
            mm(pa[:, 320:384], ones_f[:], g2)
            Gp = pa[:, 256:320].rearrange("p (t h) -> p t h", h=4)
            Glp = pa[:, 320:384].rearrange("p (t h) -> p t h", h=4)
            dbg("b5", gsb[:])
            cp("dve", G_sb[:], Gp)
            dbg("b6", gsb[:])
            act(eG[:], G_sb[:], AF.Exp)
            dbg("b7", gsb[:])
            cp("dve", eGl[:], Glp)
            act(eGl[:], eGl[:], AF.Exp)
            dbg("b8", gsb[:])
            tt("dve", tmp64[:], Glp, G_sb[:], ALU.subtract)
            dbg("b9", gsb[:])
            act(ekd[:], tmp64[:], AF.Exp)
            dbg("b10", gsb[:])
            tt("dve", bG[:], beta[:], eG[:], ALU.mult)
            dbg("beta", beta[:])
            dbg("gsb", gsb[:])
            dbg("G_sb", G_sb[:])
            for tb in range(4):
                p = pj[tb % 2]
                for k in range(KC):
                    mm(p[0:16, :], w8bf[:, k, 8:24], xT[:, k, tb * 512:(tb + 1) * 512], start=(k == 0), stop=(k == KC - 1))
                cp("act", grT[:, tb * 512:(tb + 1) * 512], p[0:16, :])
            dbg("grT", grT[:])

            def b4(ap3):
                return ap3.broadcast_to([128, GT, 128])
            v3 = lambda p_, g=GT: p_.rearrange("p (g c) -> p g c", g=g)
            id4 = ident[:].unsqueeze(1).broadcast_to([128, GT, 128])
            sil = [L1[:].rearrange("p g c -> p (g c)"), L2[:].rearrange("p g c -> p (g c)")]
            HG = GT // 2

            def rec_a(h, t, g, kd, at):
                r = t % NS
                if t == 0:
                    cp("dve", vnew[r][:], u_sb[:, g, :])
                else:
                    mm(pr[:, 0:128], wT_sb[:, g, :], S_bf[:])
                    tt("dve", vnew[r][:], u_sb[:, g, :], pr[:, 0:128], ALU.subtract)

            def rec_b(h, t, g, kd, at):
                r = t % NS
                tsl = slice(t * 128, (t + 1) * 128)
                oslot = po[:, g * 128:(g + 1) * 128]
                if t == 0:
                    mm(oslot, vnew[r][:], at[:, g, :])
                else:
                    mm(oslot, S_bf[:], qk[:, 0, tsl], start=True, stop=False)
                    mm(oslot, vnew[r][:], at[:, g, :], start=False, stop=True)
                mm(pr[:, 128:256], kd[:, g, :], vnew[r][:])
                if t == 0:
                    cp("dve", S32[:], pr[:, 128:256])
                else:
                    stt(S32[:], S32[:], eGl[:, t, h:h + 1], pr[:, 128:256], ALU.mult, ALU.add)
                cp("act", S_bf[:], S32[:])
                if g == GT - 1:
                    cp("act", raw[:, 3 + (t - GT + 1) * 128:3 + (t + 1) * 128], po[:, :])

            for h in range(4):
                qh = qk[:, 0, :]
                kh = qk[:, 1, :]
                def part_a(hh, ci):
                    c0 = ci * 512 + hh * 128
                    wb = load_w(w_in_l[:, :, c0:c0 + 128], 128)
                    w4 = convw[:, ci * 4 + hh, :]
                    for j in range(4):
                        ts("pool", dgw[:, j, :], ident[:], w4[:, j:j + 1], ALU.mult, 1.0, ALU.mult)

                    def ev(tb, p):
                        cp("dve", rawr[:, 3 + tb * 512:3 + (tb + 1) * 512], p)
                    return proj_steps(wb, 128, ev)

                def conv_b(ci, dst):
                    for tb in range(4):
                        sl = slice(tb * 512, (tb + 1) * 512)
                        p = pr if tb % 2 == 0 else po
                        for j in range(4):
                            mm(p[:, :], dgw[:, j, :], rawr[:, j + tb * 512:j + tb * 512 + 512], start=(j == 0), stop=(j == 3))
                        act(dst[:, sl] if ci == 2 else acc[:, sl], p[:, :], AF.Silu)

                def l2n(ci, dst):
                    def outf(tb, rkb):
                        sl = slice(tb * 512, (tb + 1) * 512)
                        if ci == 0:
                            stt(dst[:, sl], acc[:, sl], 128.0 ** -0.5, rkb, ALU.mult, ALU.mult)
                        else:
                            tt("dve", dst[:, sl], acc[:, sl], rkb, ALU.mult)
                    return rms_steps(lambda tb: acc[:, tb * 512:(tb + 1) * 512], 1.0, outf, sil)

                def proj_z():
                    wb = load_w(w_in_l[:, :, 1536 + h * 128:1536 + (h + 1) * 128], 128)

                    def evz(tb, p):
                        act(zs[:, tb * 512:(tb + 1) * 512], p, AF.Silu)
                    proj_fm(wb, 128, evz)

                if h == 0:
                    interleave(part_a(h, 0))
                conv_b(0, qh)
                interleave(part_a(h, 1), l2n(0, qh))
                conv_b(1, kh)
                interleave(part_a(h, 2), l2n(1, kh))
                conv_b(2, vT[:])
                proj_z()
                if h == 3:
                    cur["wob"] = outproj_w(0, acc[:].bitcast(BF16)[:, 0:4096])
                if h == 0:
                    dbg("qhat", qh)
                    dbg("khat", kh)
                    dbg("vT", vT[:])
                    dbg("zs", zs[:])

                sqs = (sq[:], sq2[:])

                def ep_sq(tb):
                    act(sqs[tb % 2], raw[:, 3 + tb * 512:3 + (tb + 1) * 512], AF.Square)

                def ep_mm(tb):
                    mm(pr[:, :], ones_bf[:], sqs[tb % 2])

                def ep_fin(tb, h=h):
                    sl = slice(tb * 512, (tb + 1) * 512)
                    rkb = sil[tb % 2]
                    act(rkb, pr[:, :], AF.Ln, bias=EPS, scale=1.0 / 128.0)
                    act(rkb, rkb, AF.Exp, scale=-0.5)
                    tt("dve", rkb, rkb, raw[:, 3 + tb * 512:3 + (tb + 1) * 512], ALU.mult)
                    stt(oT_all[:, h, sl], rkb, dng[:, 0:1], zs[:, sl], ALU.mult, ALU.mult)

                pend = None
                for tg in range(NT // GT):
                    t0 = tg * GT
                    gsl = slice(t0 * 128, (t0 + GT) * 128)
                    kdec = kdec2[tg % 2]
                    attnT = attnT2[tg % 2]
                    gcol4 = b4(gsb[:, t0:t0 + GT, h:h + 1])
                    tt("pool", L2[:], maskGT[:].unsqueeze(1).broadcast_to([128, GT, 128]), gcol4, ALU.mult)
                    tt("pool", L1[:], maskLE[:].unsqueeze(1).broadcast_to([128, GT, 128]), gcol4, ALU.mult)
                    for g in range(GT):
                        c = slice(g * 128, (g + 1) * 128)
                        mm(pc[:, c], maskLE[:], L2[:, g, :], start=True, stop=False)
                        mm(pc[:, c], ident_bf[:], negSL[:], start=False, stop=True)
                    for g in range(GT):
                        tsl = slice((t0 + g) * 128, (t0 + g + 1) * 128)
                        mm(pa[:, g * 128:(g + 1) * 128], kh[:, tsl], kh[:, tsl])
                    for g in range(GT):
                        c = slice(g * 128, (g + 1) * 128)
                        mm(pj[0][:, c], L2[:, g, :], maskLE[:], start=True, stop=False)
                        mm(pj[0][:, c], ident_bf[:], negUI[:], start=False, stop=True)
                    for g in range(GT):
                        c = slice(g * 128, (g + 1) * 128)
                        mm(pj[1][:, c], ones_f[:], L1[:, g, :])
                    act(E_sb[:], v3(pc[:, :]), AF.Exp)
                    act(ET_sb[:], v3(pj[0][:, :]), AF.Exp)
                    act(eGB[:], v3(pj[1][:, :]), AF.Exp)
                    tt("dve", E_sb[:], E_sb[:], b4(beta[:, t0:t0 + GT, h:h + 1]), ALU.mult)
                    Ck = CB[0]
                    v4 = lambda p_: p_.rearrange("p (h g c) -> p h g c", h=2, g=HG)
                    CkC = lambda X, g: X[:, g // HG, 0, g % HG, :]
                    CkB = lambda X, g: X[:, g // HG, 1, g % HG, :]
                    tt("dve", Ck[:, :, 0], v4(pa[:, :]), v4(E_sb[:].rearrange("p g c -> p (g c)")), ALU.mult)
                    for g in range(GT):
                        tr(pa[:, g * 128:(g + 1) * 128], CkC(Ck, g), ident[:])
                    for g in range(GT):
                        tsl = slice((t0 + g) * 128, (t0 + g + 1) * 128)
                        c = slice(g * 128, (g + 1) * 128)
                        mm(pb[:, c], kh[:, tsl], qh[:, tsl])
                        tr(pt[:, c], kh[:, tsl], ident_bf[:])
                        tr(pt[:, 512 + g * 128:512 + (g + 1) * 128], vT[:, tsl], ident_bf[:])
                    cp("act", Ck[:, :, 1], v4(pa[:, :]))
                    tt("dve", attnT[:], v3(pb[:, :]), ET_sb[:], ALU.mult)
                    tt("dve", Pg[:], id4, v3(pa[:, :]), ALU.subtract)
                    tt("dve", kbg[:], v3(pt[:, 0:512]), b4(bG[:, t0:t0 + GT, h:h + 1]), ALU.mult)
                    tt("dve", kdec[:], v3(pt[:, 0:512]), b4(ekd[:, t0:t0 + GT, h:h + 1]), ALU.mult)
                    tt("dve", vb[:], v3(pt[:, 512:1024]), b4(beta[:, t0:t0 + GT, h:h + 1]), ALU.mult)
                    qg3 = qh[:, gsl].rearrange("p (g c) -> p g c", g=GT)
                    tt("pool", qg3, qg3, eGB[:], ALU.mult)
                    bankCB = (pb, pc)
                    bankP = (pj[0], pj[1])
                    for lev in range(1, 8):
                        Cn = CB[lev % 2]
                        if tg == NT // GT - 1 and 5 <= lev <= 7:
                            if lev >= 6:
                                ep_fin(lev - 6)
                            ep_sq(lev - 5)
                        for hf in range(2):
                            gs = range(hf * HG, (hf + 1) * HG)
                            pcb = bankCB[hf]
                            pp_ = bankP[hf]
                            if lev <= 6:
                                for gi, g in enumerate(gs):
                                    mm(pcb[:, gi * 128:(gi + 1) * 128], CkB(Ck, g), CkC(Ck, g))
                                if lev <= 5:
                                    for gi, g in enumerate(gs):
                                        mm(pcb[:, 256 + gi * 128:256 + (gi + 1) * 128], CkC(Ck, g), CkB(Ck, g))
                            if lev >= 2:
                                for gi, g in enumerate(gs):
                                    mm(pp_[:, gi * 128:(gi + 1) * 128], CkC(Ck, g), Pg[:, g, :])
                            if lev <= 5:
                                cp("act", Cn[:, hf], pcb[:, :].rearrange("p (a g c) -> p a g c", a=2, g=HG))
                            elif lev == 6:
                                cp("act", Cn[:, hf, 0], v3(pcb[:, 0:256], HG))
                            if lev >= 2:
                                tt("dve", Pg[:, hf * HG:(hf + 1) * HG, :], Pg[:, hf * HG:(hf + 1) * HG, :],
                                   v3(pp_[:, 0:256], HG), ALU.add)
                            if pend is not None and 1 <= lev <= GT:
                                (rec_a if hf == 0 else rec_b)(*pend[lev - 1])
                            if tg == NT // GT - 1 and hf == 1 and 5 <= lev <= 7:
                                ep_mm(lev - 5)
                        Ck = Cn
                    cp("act", TT_bf[:], Pg[:])
                    if tg == NT // GT - 1:
                        ep_fin(2)
                    for g in range(GT):
                        c = slice(g * 128, (g + 1) * 128)
                        mm(pb[:, c], TT_bf[:, g, :], vb[:, g, :])
                        mm(pc[:, c], kbg[:, g, :], TT_bf[:, g, :])
                    cp("act", u_sb[:], v3(pb[:, :]))
                    cp("dve", wT_sb[:], v3(pc[:, :]))
                    pend = [(h, t0 + g, g, kdec, attnT) for g in range(GT)]
                    if tg == NT // GT - 1:
                        for args in pend:
                            rec_a(*args)
                            rec_b(*args)
                        pend = None
                    if h == 0 and tg == 0:
                        dbg("TT0", TT_bf[:, 0, :])
                def ep_last():
                    ep_sq(3)
                    ep_mm(3)
                    yield
                    ep_fin(3)
                    yield
                if h < 3:
                    interleave(ep_last(), part_a(h + 1, 0))
                else:
                    interleave(ep_last())
                if h == 0:
                    dbg("o_raw0", raw[:, 3:3 + T])
                    dbg("oT0", oT_all[:, 0, :])

            dbg("gdn_done", gsb[:])
            outproj(0, True)
            dbg("op0", gsb[:])

            lf = acc[:, 0:NT * 64].rearrange("p (t d) -> p t d", d=64)
            elast = tmp64[0:64].rearrange("p t h -> p (t h)")[:, 0:NT]
            vtm = vT[:].rearrange("p (t d) -> p t d", d=128)
            oTg = raw[:, 3:3 + T]
            for h in range(4):
                qh = qk[0:64, 0, :]
                kh = qk[0:64, 1, :]
                def gate_lf(hh):
                    for half in range(2):
                        p = pj[half]
                        for j in range(8):
                            t = half * 8 + j
                            mm(p[:, j * 64:(j + 1) * 64], grT[:, t * 128:(t + 1) * 128], w2_sb[:, hh * 64:(hh + 1) * 64],
                               start=True, stop=False)
                            mm(p[:, j * 64:(j + 1) * 64], ones_f[0:1, :], gb_sb[0:1, hh * 64:(hh + 1) * 64], start=False, stop=True)
                        act(acc[:, half * 512:(half + 1) * 512], p[:, :], AF.Exp, scale=-1.0)
                    act(acc[:, 0:NT * 64], acc[:, 0:NT * 64], AF.Ln, bias=1.0)

                if h == 0:
                    gate_lf(0)
                    dbg("lf", acc[:, 0:NT * 64])
                def qk_steps(hh):
                    wq = load_w(w_in_l[:, :, 2056 + hh * 64:2056 + (hh + 1) * 64], 64)
                    wk = load_w(w_in_l[:, :, 2312 + hh * 64:2312 + (hh + 1) * 64], 64)

                    def gen():
                        for tb in range(4):
                            sl = slice(tb * 512, (tb + 1) * 512)
                            for j in range(4):
                                t = tb * 4 + j
                                mm(pc[0:64, j * 128:(j + 1) * 128], lf[:, t, :], maskLE[:])
                            act(ebuf[0], pc[0:64, :], AF.Exp, scale=-1.0 / 16.0)
                            act(ebuf[1], pc[0:64, :], AF.Exp, scale=1.0 / 16.0)
                            p = pj[0]
                            for k in range(KC):
                                mm(p[0:64, :], wq[:, k, 0:64], xT[:, k, sl], start=(k == 0), stop=(k == KC - 1))
                            stt(qh[:, sl], p[0:64, :], 0.125, ebuf[0], ALU.mult, ALU.mult)
                            p = pj[1]
                            for k in range(KC):
                                mm(p[0:64, :], wk[:, k, 0:64], xT[:, k, sl], start=(k == 0), stop=(k == KC - 1))
                            tt("dve", kh[:, sl], p[0:64, :], ebuf[1], ALU.mult)
                            for j in range(4):
                                t = tb * 4 + j
                                cp("dve", elast[:, t:t + 1], ebuf[0][:, j * 128 + 127:j * 128 + 128])
                            yield
                    return gen()

                if h == 0:
                    interleave(qk_steps(0))
                wz = load_w(w_in_l[:, :, 3080 + h * 128:3080 + (h + 1) * 128], 128)

                def evz2(tb, p):
                    act(zs[:, tb * 512:(tb + 1) * 512], p, AF.Silu)
                proj_fm(wz, 128, evz2)
                wv = load_w(w_in_l[:, :, 2568 + h * 128:2568 + (h + 1) * 128], 128)
                for t in range(NT):
                    p = pj[t % 2]
                    for k in range(KC):
                        mm(p[:, 0:128], xT[:, k, t * 128:(t + 1) * 128], wv[:, k, :], start=(k == 0), stop=(k == KC - 1))
                    cp("act" if t % 2 == 0 else "dve", vtm[:, t, :], p[:, 0:128])
                if h == 0:
                    dbg("gq", qh)
                    dbg("gk", kh)
                    dbg("gv", vT[:])
                if h == 3:
                    cur["wob"] = outproj_w(1, acc[:].bitcast(BF16)[:, 0:4096])
                m01b = m01UI[:].unsqueeze(1).broadcast_to([128, 4, 128])
                for tg in range(NT // 4):
                    t0 = tg * 4
                    kt4 = kbg if tg % 2 == 0 else vb
                    at4 = attnT2[tg % 2]
                    for g in range(4):
                        tsl = slice((t0 + g) * 128, (t0 + g + 1) * 128)
                        tr(pt[:, g * 64:(g + 1) * 64], kh[:, tsl], ident_bf[0:64, 0:64])
                        mm(pa[:, g * 128:(g + 1) * 128], kh[:, tsl], qh[:, tsl])
                    cp("act", kt4[:, :, 0:64], pt[:, 0:256].rearrange("p (g c) -> p g c", g=4))
                    tt("dve", at4[:], pa[:, :].rearrange("p (g c) -> p g c", g=4), m01b, ALU.mult)
                    for g in range(4):
                        mm(pb[0:64, g * 128:(g + 1) * 128], kt4[:, g, 0:64], vtm[:, t0 + g, :])
                    for g in range(4):
                        t = t0 + g
                        tsl = slice(t * 128, (t + 1) * 128)
                        Rn = Pg[0:64, t % 4, :]
                        Rp = Pg[0:64, (t - 1) % 4, :]
                        Sn = wT_sb[0:64, t % 4, :]
                        Sp = wT_sb[0:64, (t - 1) % 4, :]
                        oslot = po[:, g * 128:(g + 1) * 128]
                        if t == 0:
                            mm(oslot, vtm[:, t, :], at4[:, g, :])
                            cp("dve", Rn, pb[0:64, 0:128])
                        else:
                            mm(oslot, Sp, qh[:, tsl], start=True, stop=False)
                            mm(oslot, vtm[:, t, :], at4[:, g, :], start=False, stop=True)
                            stt(Rn, Rp, elast[:, t - 1:t], pb[0:64, g * 128:(g + 1) * 128], ALU.mult, ALU.add)
                        act(Sn, Rn, AF.Copy, scale=elast[:, t:t + 1])
                    cp("act", oTg[:, t0 * 128:(t0 + 4) * 128], po[:, :])
                if h < 3:
                    gate_lf(h + 1)
                def outf_g(tb, rkb, h=h):
                    sl = slice(tb * 512, (tb + 1) * 512)
                    tt("dve", rkb, rkb, oTg[:, sl], ALU.mult)
                    stt(oT_all[:, h, sl], rkb, glg[:, 0:1], zs[:, sl], ALU.mult, ALU.mult)
                epg = rms_steps(lambda tb: oTg[:, tb * 512:(tb + 1) * 512], 1.0 / 128.0, outf_g,
                                (rk, Pg[:].rearrange("p g c -> p (g c)")))
                if h < 3:
                    interleave(qk_steps(h + 1), epg)
                else:
                    interleave(epg)
                if h == 0:
                    dbg("o_raw4", oTg)
                    dbg("oT4", oT_all[:, 0, :])
            lng_t = CB[0][:].rearrange("p a b g c -> p (a b g c)")
            lnb_t = CB[1][:].rearrange("p a b g c -> p (a b g c)")
            s.dma("sp", lng_t, lng_d[l].partition_broadcast(128))
            s.dma("sp", lnb_t, lnb_d[l].partition_broadcast(128))
            outproj(1, False, ln=(lng_t, lnb_t, last))
            dbg("xout", x_tm[:, 0, :])
          except _Stop:
            s.dma("sp", y_d[0:128, :], x_tm[:, 0, :])
            break
        s.emit()
    return nc, dbg_out


_CACHE = {}


def _prep_inputs(inputs, b):
    f = lambda a: np.ascontiguousarray(np.asarray(a, dtype=np.float32))
    m = {
        "x": f(inputs["x"][b]),
        "w_in": f(inputs["w_in"]),
        "conv_wt": f(np.transpose(np.asarray(inputs["conv_w"]), (0, 2, 1))),
        "dn_a_log": f(inputs["dn_a_log"]),
        "dn_dt_bias": f(inputs["dn_dt_bias"]),
        "gla_gate_w2": f(inputs["gla_gate_w2"]),
        "gla_gate_b": f(inputs["gla_gate_b"]),
        "dn_norm_g": f(inputs["dn_norm_g"]),
        "gla_norm_g": f(inputs["gla_norm_g"]),
        "w_out": f(inputs["w_out"]),
        "ln_g": f(inputs["ln_g"]),
        "ln_b": f(inputs["ln_b"]),
    }
    return m


FUSED = True


def kernel(**inputs):
    in_maps = [_prep_inputs(inputs, b) for b in range(8)]
    groups = [list(range(DEPTH))] if FUSED else [[l] for l in range(DEPTH)]
    for grp in groups:
        key = tuple(grp)
        if key not in _CACHE:
            _CACHE[key] = build(grp)[0]
        res = run_bass_kernel_spmd(_CACHE[key], in_maps, core_ids=list(range(8)))
        ys = [np.asarray(r["y"]) for r in res.results]
        for b in range(8):
            in_maps[b]["x"] = np.ascontiguousarray(ys[b], dtype=np.float32)
    return np.stack(ys, axis=0).astype(np.float32)
```

```python
import contextlib
import numpy as np
import concourse.bass as bass
import concourse.mybir as mybir
from concourse.bass_utils import run_bass_kernel_spmd

F32 = mybir.dt.float32
BF16 = mybir.dt.bfloat16
ALU = mybir.AluOpType
AF = mybir.ActivationFunctionType

T = 2048
D = 1024
NT = 16
KC = 8
DEPTH = 2
D_IN = 3608
ALPHA = (2.0 * DEPTH) ** 0.25
EPS = 1e-6
NEG = -30000.0

NDMA = 12


def _region(ap):
    sp = str(ap.space).upper()
    if "DRAM" in sp:
        return None
    if "PSUM" in sp:
        return (ap.tensor.name, 0, 128, 0, 1 << 30)
    pat = ap.ap
    pstride, pn = pat[0]
    off = int(ap.offset)
    if pstride == 0:
        p_lo, f_lo, pn = 0, off, 1
    else:
        p_lo, f_lo = off // pstride, off % pstride
    ext = 1
    for st, cnt in pat[1:]:
        ext += abs(st) * (cnt - 1)
    esz = mybir.dt.size(ap.dtype)
    return (ap.tensor.name, p_lo, p_lo + pn, f_lo * esz, (f_lo + ext) * esz)


class Sched:
    ENG = ("pe", "dve", "act", "pool", "sp")

    def __init__(self, nc):
        self.nc = nc
        self.prog = {e: [] for e in self.ENG}
        self.cnt = {e: 0 for e in self.ENG}
        self.seen = {e: {} for e in self.ENG}
        self.dcnt = [0] * NDMA
        self.dnext = 0
        self.recs = {}
        self.out_dmas = []

    def _deps(self, reads, writes):
        deps = {}
        for lst, only_w in ((reads, True), (writes, False)):
            for ap in lst:
                r = _region(ap)
                if r is None:
                    continue
                for rec in self.recs.get(r[0], ()):
                    if only_w and not rec[6]:
                        continue
                    if rec[0] < r[2] and r[1] < rec[1] and rec[2] < r[4] and r[3] < rec[3]:
                        if deps.get(rec[4], 0) < rec[5]:
                            deps[rec[4]] = rec[5]
        return deps

    def _record(self, reads, writes, src, val):
        for ap in writes:
            r = _region(ap)
            if r is None:
                continue
            lst = self.recs.setdefault(r[0], [])
            lst[:] = [x for x in lst if not (r[1] <= x[0] and x[1] <= r[2] and r[3] <= x[2] and x[3] <= r[4])]
            lst.append([r[1], r[2], r[3], r[4], src, val, True])
        for ap in reads:
            r = _region(ap)
            if r is None:
                continue
            lst = self.recs.setdefault(r[0], [])
            for x in lst:
                if (not x[6]) and x[4] == src and x[0] == r[1] and x[1] == r[2] and x[2] == r[3] and x[3] == r[4]:
                    x[5] = max(x[5], val)
                    break
            else:
                lst.append([r[1], r[2], r[3], r[4], src, val, False])

    def _waits(self, eng, deps):
        out = []
        seen = self.seen[eng]
        for src, val in deps.items():
            if src == eng and eng == "pe":
                continue
            if seen.get(src, 0) >= val:
                continue
            seen[src] = val
            out.append((src, val))
        return out

    @staticmethod
    def _excl(reads, writes):
        ps = [a for a in reads if "PSUM" in str(a.space).upper()]
        if ps:
            reads = [a for a in reads if "PSUM" not in str(a.space).upper()]
            writes = list(writes) + ps
        return reads, writes

    def op(self, eng, fn, reads=(), writes=()):
        reads, writes = self._excl(list(reads), list(writes))
        deps = self._deps(reads, writes)
        idx = self.cnt[eng] + 1
        self.cnt[eng] = idx
        waits = self._waits(eng, deps)
        self.prog[eng].append((waits, fn, (eng, 1)))
        self._record(reads, writes, eng, idx)
        return idx

    def dma(self, queue, out, in_, **kw):
        deps = self._deps([in_], [out])
        c = self.dnext
        self.dnext = (self.dnext + 1) % NDMA
        src = "dma%d" % c
        if self.dcnt[c] > 0 and deps.get(src, 0) < 16 * self.dcnt[c]:
            deps[src] = 16 * self.dcnt[c]
        self.dcnt[c] += 1
        val = 16 * self.dcnt[c]
        waits = self._waits(queue, deps)

        def fn(e, out=out, in_=in_, kw=kw):
            return e.dma_start(out=out, in_=in_, **kw)

        self.prog[queue].append((waits, fn, (src, 16)))
        self._record([in_], [out], src, val)
        if _region(out) is None:
            self.out_dmas.append((src, val))

    def emit(self):
        nc = self.nc
        with contextlib.ExitStack() as st:
            sems = {}
            for e in self.ENG:
                sems[e] = st.enter_context(nc.semaphore("s_" + e))
            for c in range(NDMA):
                sems["dma%d" % c] = st.enter_context(nc.semaphore("s_dma%d" % c))
            fin = {}
            for src, val in self.out_dmas:
                fin[src] = max(fin.get(src, 0), val)
            block = st.enter_context(nc.Block())
            handles = {"pe": "tensor", "dve": "vector", "act": "scalar", "pool": "gpsimd", "sp": "sync"}

            def make(ename):
                prog = self.prog[ename]

                def body(e):
                    for waits, fn, (isrc, inc) in prog:
                        for src, val in waits:
                            e.wait_ge(sems[src], val)
                        fn(e).then_inc(sems[isrc], inc)
                    if ename == "sp":
                        for src, val in fin.items():
                            e.wait_ge(sems[src], val)
                return body

            for ename in self.ENG:
                getattr(block, handles[ename])(make(ename))


class _Stop(Exception):
    pass


def build(layers, debug=(), stop=None):
    nc = bass.Bass("TRN2", target_bir_lowering=False)
    dram = {}

    def din(name, shape):
        dram[name] = nc.dram_tensor(name, list(shape), F32, kind="ExternalInput").ap()
        return dram[name]

    x_d = din("x", [T, D])
    w_in_d = din("w_in", [DEPTH, D, D_IN])
    conv_d = din("conv_wt", [DEPTH, 1536, 4])
    alog_d = din("dn_a_log", [DEPTH, 4])
    dtb_d = din("dn_dt_bias", [DEPTH, 4])
    w2_d = din("gla_gate_w2", [DEPTH, 16, 256])
    gb_d = din("gla_gate_b", [DEPTH, 256])
    dng_d = din("dn_norm_g", [DEPTH, 128])
    glg_d = din("gla_norm_g", [DEPTH, 128])
    w_out_d = din("w_out", [DEPTH, D, D])
    lng_d = din("ln_g", [DEPTH, D])
    lnb_d = din("ln_b", [DEPTH, D])
    y_d = nc.dram_tensor("y", [T, D], F32, kind="ExternalOutput").ap()
    dbg_out = {}

    with contextlib.ExitStack() as st:
        def sb(name, shape, dt=F32):
            return st.enter_context(nc.sbuf_tensor(name, list(shape), dt))

        def ps(name, shape, dt=F32):
            return st.enter_context(nc.psum_tensor(name, list(shape), dt))

        s = Sched(nc)

        x_tm = sb("x_tm", [128, NT, D])
        xT = sb("xT", [128, KC, T], BF16)
        oT_all = sb("oT_all", [128, 4, T], BF16)
        wst = [sb("wst%d" % i, [128, KC, 128]) for i in range(2)]
        wbf = [sb("wbf%d" % i, [128, KC, 128], BF16) for i in range(2)]
        raw = sb("raw", [128, 3 + T])
        acc = sb("acc", [128, T])
        qk = sb("qk", [128, 2, T], BF16)
        vT = sb("vT", [128, T], BF16)
        zs = sb("zs", [128, T], BF16)
        ident = sb("ident", [128, 128])
        ident_bf = sb("ident_bf", [128, 128], BF16)
        maskLE = sb("maskLE", [128, 128])
        maskGT = sb("maskGT", [128, 128])
        ones_f = sb("ones_f", [128, 128])
        ones_bf = sb("ones_bf", [128, 128], BF16)
        negSL = sb("negSL", [128, 128], BF16)
        negUI = sb("negUI", [128, 128], BF16)
        m01UI = sb("m01UI", [128, 128])
        convw = sb("convw", [128, 12, 4])
        alog_b = sb("alog_b", [128, 4])
        dtb_b = sb("dtb_b", [128, 4])
        nA_b = sb("nA_b", [128, 4])
        w2_sb = sb("w2_sb", [16, 256], BF16)
        gb_sb = sb("gb_sb", [1, 256])
        dng = sb("dng", [128, 1])
        glg = sb("glg", [128, 1])
        w8st = sb("w8st", [128, KC, 24])
        w8bf = sb("w8bf", [128, KC, 24], BF16)
        grT = sb("grT", [16, T], BF16)
        beta = sb("beta", [128, NT, 4])
        gsb = sb("gsb", [128, NT, 4])
        G_sb = sb("G_sb", [128, NT, 4])
        eG = sb("eG", [128, NT, 4])
        bG = sb("bG", [128, NT, 4])
        eGl = sb("eGl", [128, NT, 4])
        ekd = sb("ekd", [128, NT, 4])
        tmp64 = sb("tmp64", [128, NT, 4])
        NS = 2
        GT = 4
        L2 = sb("L2g", [128, GT, 128])
        L1 = sb("L1g", [128, GT, 128])
        E_sb = L2
        eGB = L1
        Pg = sb("Pg", [128, GT, 128])
        ET_sb = Pg
        CB = [sb("CB%d" % j, [128, 2, 2, GT // 2, 128]) for j in range(2)]
        TT_bf = sb("TTg", [128, GT, 128], BF16)
        kbg = sb("kbg", [128, GT, 128], BF16)
        kdec2 = [sb("kdec%d" % i, [128, GT, 128], BF16) for i in range(2)]
        vb = sb("vbg", [128, GT, 128], BF16)
        attnT2 = [sb("attnTg%d" % i, [128, GT, 128], BF16) for i in range(2)]
        wT_sb = sb("wTg", [128, GT, 128], BF16)
        u_sb = sb("ug", [128, GT, 128], BF16)
        vnew = [sb("vnew_%d" % i, [128, 128], BF16) for i in range(NS)]
        sq2 = sb("sq2", [128, 512], BF16)
        S32 = sb("S32", [128, 128])
        S_bf = sb("S_bf", [128, 128], BF16)
        sq = sb("sq", [128, 512], BF16)
        scr = sb("scr", [128, 512])
        rk = scr[:, 0:512]
        ebuf = [L1[0:64].rearrange("p g c -> p (g c)"), L2[0:64].rearrange("p g c -> p (g c)")]
        rawr = sb("rawr", [128, 4 + T], mybir.dt.float32r)
        dgw = sb("dgw", [128, 4, 128], mybir.dt.float32r)
        stat = sb("stat", [128, 64])

        pj = [ps("pj%d" % i, [128, 512]) for i in range(2)]
        pa = ps("pa", [128, 512])
        pb = ps("pb", [128, 512])
        pt = ps("pt", [128, 1024], BF16)
        pc = ps("pc", [128, 512])
        pr = ps("pr", [128, 512])
        po = ps("po", [128, 512])

        def mm(out, lhsT, rhs, start=True, stop=True):
            rd = [lhsT, rhs] + ([] if start else [out])
            s.op("pe", lambda e: e.matmul(out, lhsT=lhsT, rhs=rhs, start=start, stop=stop), reads=rd, writes=[out])

        def tr(out, in_, idn):
            s.op("pe", lambda e: e.transpose(out, in_, idn), reads=[in_, idn], writes=[out])

        def act(out, in_, func, bias=None, scale=None, accum=None):
            kw = {}
            rd = [in_]
            if bias is not None:
                kw["bias"] = bias
                if not isinstance(bias, float):
                    rd.append(bias)
            if scale is not None:
                kw["scale"] = scale
                if not isinstance(scale, float):
                    rd.append(scale)
            wr = [out]
            if accum is not None:
                kw["accum_out"] = accum
                wr.append(accum)
            s.op("act", lambda e: e.activation(out=out, in_=in_, func=func, **kw), reads=rd, writes=wr)

        def tt(eng, out, in0, in1, op):
            s.op(eng, lambda e: e.tensor_tensor(out=out, in0=in0, in1=in1, op=op), reads=[in0, in1], writes=[out])

        def ts(eng, out, in0, s1, op0, s2=None, op1=None):
            rd = [in0] + [x for x in (s1, s2) if x is not None and not isinstance(x, float)]
            if op1 is None:
                s.op(eng, lambda e: e.tensor_scalar(out=out, in0=in0, scalar1=s1, scalar2=None, op0=op0), reads=rd, writes=[out])
            else:
                s.op(eng, lambda e: e.tensor_scalar(out=out, in0=in0, scalar1=s1, scalar2=s2, op0=op0, op1=op1), reads=rd, writes=[out])

        def stt(out, in0, sc, in1, op0, op1):
            rd = [in0, in1] + ([] if isinstance(sc, float) else [sc])
            s.op("dve", lambda e: e.scalar_tensor_tensor(out=out, in0=in0, scalar=sc, in1=in1, op0=op0, op1=op1), reads=rd, writes=[out])

        def cp(eng, out, in_):
            if eng == "act":
                s.op("act", lambda e: e.copy(out=out, in_=in_), reads=[in_], writes=[out])
            else:
                s.op(eng, lambda e: e.tensor_copy(out=out, in_=in_), reads=[in_], writes=[out])

        def memset(ap, v):
            s.op("pool", lambda e: e.memset(ap, v), writes=[ap])

        def asel(out, in_, base, cm, step, cmp, fill):
            s.op("pool", lambda e: e.affine_select(out=out, in_=in_, pattern=[[step, 128]], base=base,
                                                   channel_multiplier=cm, compare_op=cmp, fill=fill),
                 reads=[in_], writes=[out])

        def dbg(name, ap):
            if stop == name:
                raise _Stop()
            if name in debug:
                shp = list(ap.shape)
                d = nc.dram_tensor("dbg_" + name, shp, ap.dtype, kind="ExternalOutput").ap()
                dbg_out[name] = d
                s.dma("sp", d, ap)

        wstate = {"st": 0, "bf": 0}

        def load_w(src_ap, ncols):
            a = wst[wstate["st"] % 2]
            b = wbf[wstate["bf"] % 2]
            wstate["st"] += 1
            wstate["bf"] += 1
            s.dma("sp", a[:, :, 0:ncols], src_ap)
            cp("pool", b[:, :, 0:ncols], a[:, :, 0:ncols])
            return b

        def proj_fm(wb, M, evac):
            for tb in range(4):
                p = pj[tb % 2]
                for k in range(KC):
                    mm(p[0:M, :], wb[:, k, 0:M], xT[:, k, tb * 512:(tb + 1) * 512], start=(k == 0), stop=(k == KC - 1))
                evac(tb, p[0:M, :])

        def ln_stats(t0, n):
            st_ = stat[:, ((t0 // n) % 2) * 32:((t0 // n) % 2) * 32 + 32].rearrange("p (a j) -> p a j", j=4)
            zo = raw[:, 3:3 + D]
            for j in range(n):
                z = x_tm[:, t0 + j, :]
                act(zo, z, AF.Identity, accum=st_[:, 0, j:j + 1])
                act(zo, z, AF.Square, accum=st_[:, 1, j:j + 1])
            ts("dve", st_[:, 2, 0:n], st_[:, 0, 0:n], 1.0 / D, ALU.mult)
            tt("dve", st_[:, 3, 0:n], st_[:, 2, 0:n], st_[:, 2, 0:n], ALU.mult)
            stt(st_[:, 4, 0:n], st_[:, 1, 0:n], 1.0 / D, st_[:, 3, 0:n], ALU.mult, ALU.subtract)
            act(st_[:, 5, 0:n], st_[:, 4, 0:n], AF.Ln, bias=EPS)
            act(st_[:, 5, 0:n], st_[:, 5, 0:n], AF.Exp, scale=-0.5)
            stt(st_[:, 6, 0:n], st_[:, 2, 0:n], -1.0, st_[:, 5, 0:n], ALU.mult, ALU.mult)

        def ln_apply(t0, n, lng_t, lnb_t, last):
            st_ = stat[:, ((t0 // n) % 2) * 32:((t0 // n) % 2) * 32 + 32].rearrange("p (a j) -> p a j", j=4)
            for j in range(n):
                z = x_tm[:, t0 + j, :]
                act(z, z, AF.Identity, bias=st_[:, 6, j:j + 1], scale=st_[:, 5, j:j + 1])
                tt("dve", z, z, lng_t, ALU.mult)
                tt("pool", z, z, lnb_t, ALU.add)
                if last:
                    s.dma("sp", y_d[(t0 + j) * 128:(t0 + j + 1) * 128, :], z)

        def proj_steps(wb, M, evac):
            for tb in range(4):
                p = pj[tb % 2]
                for k in range(KC):
                    mm(p[0:M, :], wb[:, k, 0:M], xT[:, k, tb * 512:(tb + 1) * 512], start=(k == 0), stop=(k == KC - 1))
                evac(tb, p[0:M, :])
                yield

        def interleave(*gens):
            gens = list(gens)
            while gens:
                for g_ in list(gens):
                    try:
                        next(g_)
                    except StopIteration:
                        gens.remove(g_)

        def rms_steps(src, ln_scale, out_fn, rkbufs):
            sqb = (sq[:], sq2[:])
            pbk = (pa, pb)

            def stage1(tb):
                act(sqb[tb % 2], src(tb), AF.Square)
                mm(pbk[tb % 2][:, :], ones_bf[:], sqb[tb % 2])

            def stage2(tb):
                rkb = rkbufs[tb % 2]
                act(rkb, pbk[tb % 2][:, :], AF.Ln, bias=EPS, scale=ln_scale)
                act(rkb, rkb, AF.Exp, scale=-0.5)
                out_fn(tb, rkb)
            stage1(0)
            stage1(1)
            yield
            for tb in range(4):
                stage2(tb)
                if tb + 2 < 4:
                    stage1(tb + 2)
                yield

        def rms_blocks(src, ln_scale, out_fn, rkbufs):
            for _ in rms_steps(src, ln_scale, out_fn, rkbufs):
                pass

        def outproj_w(rh, buf2d):
            wob = buf2d.rearrange("p (k c) -> p k c", k=4)
            for cc in range(8):
                c0 = cc * 128
                a = wst[wstate["st"] % 2]
                wstate["st"] += 1
                s.dma("sp", a[:, 0:4, :], cur["w_out_l"][:, rh * 4:(rh + 1) * 4, c0:c0 + 128])
                cp("pool", wob[:, :, c0:c0 + 128], a[:, 0:4, :])
            return wob

        def outproj(rh, first, ln=None):
            wob = cur.get("wob")
            if wob is None:
                wob = outproj_w(rh, qk[:].rearrange("p a t -> p (a t)"))
            cur["wob"] = None
            for t in range(NT):
                for half in range(2):
                    p = pj[half]
                    for e4 in range(4):
                        mm(p[:, :], oT_all[:, e4, t * 128:(t + 1) * 128], wob[:, e4, half * 512:(half + 1) * 512],
                           start=(e4 == 0), stop=(e4 == 3))
                    xs = x_tm[:, t, half * 512:(half + 1) * 512]
                    if first:
                        stt(xs, xs, ALPHA, p[:, :], ALU.mult, ALU.add)
                    else:
                        tt("dve", xs, xs, p[:, :], ALU.add)
                if ln is not None and t % 4 == 3:
                    ln_stats(t - 3, 4)
                    if t >= 7:
                        ln_apply(t - 7, 4, *ln)
            if ln is not None:
                ln_apply(NT - 4, 4, *ln)

        cur = {}
        memset(ones_f[:], 1.0)
        memset(ones_bf[:], 1.0)
        asel(ident[:], ones_f[:], 0, -1, 1, ALU.is_equal, 0.0)
        cp("pool", ident_bf[:], ident[:])
        asel(maskLE[:], ones_f[:], 0, -1, 1, ALU.is_ge, 0.0)
        asel(maskGT[:], ones_f[:], 0, 1, -1, ALU.is_gt, 0.0)
        asel(m01UI[:], ones_f[:], 0, -1, 1, ALU.is_ge, 0.0)
        memset(L1[:, 0, :], 0.0)
        asel(L2[:, 0, :], L1[:, 0, :], 0, 1, -1, ALU.is_gt, NEG)
        cp("pool", negSL[:], L2[:, 0, :])
        asel(L2[:, 1, :], L1[:, 0, :], 0, -1, 1, ALU.is_ge, NEG)
        cp("pool", negUI[:], L2[:, 1, :])
        act(rawr[:, 0:3], L1[:, 0, 0:3], AF.Copy)

        for t in range(NT):
            s.dma("sp", x_tm[:, t, :], x_d[t * 128:(t + 1) * 128, :])

        for li, l in enumerate(layers):
          try:
            last = (li == len(layers) - 1)
            w_in_l = w_in_d[l].rearrange("(k p) c -> p k c", p=128)
            w_out_l = w_out_d[l].rearrange("(k p) c -> p k c", p=128)
            cur["w_out_l"] = w_out_l

            s.dma("sp", convw[:], conv_d[l].rearrange("(n p) j -> p n j", p=128))
            s.dma("sp", alog_b[:], alog_d[l].partition_broadcast(128))
            s.dma("sp", dtb_b[:], dtb_d[l].partition_broadcast(128))
            s.dma("sp", scr[0:16, 0:256], w2_d[l])
            cp("pool", w2_sb[:], scr[0:16, 0:256])
            s.dma("sp", gb_sb[:], gb_d[l:l + 1, :])
            s.dma("sp", dng[:], dng_d[l].rearrange("(d o) -> d o", o=1))
            s.dma("sp", glg[:], glg_d[l].rearrange("(d o) -> d o", o=1))
            s.dma("sp", w8st[:, :, 0:8], w_in_l[:, :, 2048:2056])
            s.dma("sp", w8st[:, :, 8:24], w_in_l[:, :, 3592:3608])
            cp("pool", w8bf[:], w8st[:])
            act(nA_b[:], alog_b[:], AF.Exp)
            ts("dve", nA_b[:], nA_b[:], -1.0, ALU.mult)

            for t in range(NT):
                for half in range(2):
                    p = pj[(2 * t + half) % 2]
                    for j in range(4):
                        k = half * 4 + j
                        tr(p[:, j * 128:(j + 1) * 128], x_tm[:, t, k * 128:(k + 1) * 128], ident[:])
                    eng = "act" if half == 0 else "dve"
                    cp(eng, xT[:, half * 4:half * 4 + 4, t * 128:(t + 1) * 128],
                       p[:, :].rearrange("p (a b) -> p a b", a=4))
            dbg("xT", xT[:, 0, :])

            bgp = pb
            for t in range(NT):
                for k in range(KC):
                    mm(bgp[:, t * 8:(t + 1) * 8], xT[:, k, t * 128:(t + 1) * 128], w8bf[:, k, 0:8],
                       start=(k == 0), stop=(k == KC - 1))
            bg3 = bgp[:, 0:128].rearrange("p (t c) -> p t c", c=8)
            dbg("b0", xT[:, 0, 0:128])
            act(beta[:], bg3[:, :, 0:4], AF.Sigmoid)
            dbg("b1", beta[:])
            tt("dve", tmp64[:], bg3[:, :, 4:8], dtb_b[:].unsqueeze(1).broadcast_to([128, NT, 4]), ALU.add)
            dbg("b2", tmp64[:])
            act(tmp64[:], tmp64[:], AF.Exp)
            act(tmp64[:], tmp64[:], AF.Ln, bias=1.0)
            dbg("b3", tmp64[:])
            tt("dve", gsb[:], tmp64[:], nA_b[:].unsqueeze(1).broadcast_to([128, NT, 4]), ALU.mult)
            g2 = gsb[:].rearrange("p t h -> p (t h)")
            dbg("b4", gsb[:])
            mm(pa[:, 256:320], maskLE[:], g2)
            mm(pa[:, 320:384], ones_f[:], g2)
            Gp = pa[:, 256:320].rearrange("p (t h) -> p t h", h=4)
            Glp = pa[:, 320:384].rearrange("p (t h) -> p t h", h=4)
            dbg("b5", gsb[:])
            cp("dve", G_sb[:], Gp)
            dbg("b6", gsb[:])
            act(eG[:], G_sb[:], AF.Exp)
            dbg("b7", gsb[:])
            cp("dve", eGl[:], Glp)
            act(eGl[:], eGl[:], AF.Exp)
            dbg("b8", gsb[:])
            tt("dve", tmp64[:], Glp, G_sb[:], ALU.subtract)
            dbg("b9", gsb[:])
            act(ekd[:], tmp64[:], AF.Exp)
            dbg("b10", gsb[:])
            tt("dve", bG[:], beta[:], eG[:], ALU.mult)
            dbg("beta", beta[:])
            dbg("gsb", gsb[:])
            dbg("G_sb", G_sb[:])
            for tb in range(4):
                p = pj[tb % 2]
                for k in range(KC):
                    mm(p[0:16, :], w8bf[:, k, 8:24], xT[:, k, tb * 512:(tb + 1) * 512], start=(k == 0), stop=(k == KC - 1))
                cp("act", grT[:, tb * 512:(tb + 1) * 512], p[0:16, :])
            dbg("grT", grT[:])

            def b4(ap3):
                return ap3.broadcast_to([128, GT, 128])
            v3 = lambda p_, g=GT: p_.rearrange("p (g c) -> p g c", g=g)
            id4 = ident[:].unsqueeze(1).broadcast_to([128, GT, 128])
            sil = [L1[:].rearrange("p g c -> p (g c)"), L2[:].rearrange("p g c -> p (g c)")]
            HG = GT // 2

            def rec_a(h, t, g, kd, at):
                r = t % NS
                if t == 0:
                    cp("dve", vnew[r][:], u_sb[:, g, :])
                else:
                    mm(pr[:, 0:128], wT_sb[:, g, :], S_bf[:])
                    tt("dve", vnew[r][:], u_sb[:, g, :], pr[:, 0:128], ALU.subtract)

            def rec_b(h, t, g, kd, at):
                r = t % NS
                tsl = slice(t * 128, (t + 1) * 128)
                oslot = po[:, g * 128:(g + 1) * 128]
                if t == 0:
                    mm(oslot, vnew[r][:], at[:, g, :])
                else:
                    mm(oslot, S_bf[:], qk[:, 0, tsl], start=True, stop=False)
                    mm(oslot, vnew[r][:], at[:, g, :], start=False, stop=True)
                mm(pr[:, 128:256], kd[:, g, :], vnew[r][:])
                if t == 0:
                    cp("dve", S32[:], pr[:, 128:256])
                else:
                    stt(S32[:], S32[:], eGl[:, t, h:h + 1], pr[:, 128:256], ALU.mult, ALU.add)
                cp("act", S_bf[:], S32[:])
                if g == GT - 1:
                    cp("act", raw[:, 3 + (t - GT + 1) * 128:3 + (t + 1) * 128], po[:, :])

            for h in range(4):
                qh = qk[:, 0, :]
                kh = qk[:, 1, :]
                def part_a(hh, ci):
                    c0 = ci * 512 + hh * 128
                    wb = load_w(w_in_l[:, :, c0:c0 + 128], 128)
                    w4 = convw[:, ci * 4 + hh, :]
                    for j in range(4):
                        ts("pool", dgw[:, j, :], ident[:], w4[:, j:j + 1], ALU.mult, 1.0, ALU.mult)

                    def ev(tb, p):
                        cp("dve", rawr[:, 3 + tb * 512:3 + (tb + 1) * 512], p)
                    return proj_steps(wb, 128, ev)

                def conv_b(ci, dst):
                    for tb in range(4):
                        sl = slice(tb * 512, (tb + 1) * 512)
                        p = pr if tb % 2 == 0 else po
                        for j in range(4):
                            mm(p[:, :], dgw[:, j, :], rawr[:, j + tb * 512:j + tb * 512 + 512], start=(j == 0), stop=(j == 3))
                        act(dst[:, sl] if ci == 2 else acc[:, sl], p[:, :], AF.Silu)

                def l2n(ci, dst):
                    def outf(tb, rkb):
                        sl = slice(tb * 512, (tb + 1) * 512)
                        if ci == 0:
                            stt(dst[:, sl], acc[:, sl], 128.0 ** -0.5, rkb, ALU.mult, ALU.mult)
                        else:
                            tt("dve", dst[:, sl], acc[:, sl], rkb, ALU.mult)
                    return rms_steps(lambda tb: acc[:, tb * 512:(tb + 1) * 512], 1.0, outf, sil)

                def proj_z():
                    wb = load_w(w_in_l[:, :, 1536 + h * 128:1536 + (h + 1) * 128], 128)

                    def evz(tb, p):
                        act(zs[:, tb * 512:(tb + 1) * 512], p, AF.Silu)
                    proj_fm(wb, 128, evz)

                if h == 0:
                    interleave(part_a(h, 0))
                conv_b(0, qh)
                interleave(part_a(h, 1), l2n(0, qh))
                conv_b(1, kh)
                interleave(part_a(h, 2), l2n(1, kh))
                conv_b(2, vT[:])
                proj_z()
                if h == 3:
                    cur["wob"] = outproj_w(0, acc[:].bitcast(BF16)[:, 0:4096])
                if h == 0:
                    dbg("qhat", qh)
                    dbg("khat", kh)
                    dbg("vT", vT[:])
                    dbg("zs", zs[:])

                pend = None
                for tg in range(NT // GT):
                    t0 = tg * GT
                    gsl = slice(t0 * 128, (t0 + GT) * 128)
                    kdec = kdec2[tg % 2]
                    attnT = attnT2[tg % 2]
                    if tg == 0:
                        gcol4 = b4(gsb[:, t0:t0 + GT, h:h + 1])
                        tt("pool", L2[:], maskGT[:].unsqueeze(1).broadcast_to([128, GT, 128]), gcol4, ALU.mult)
                        tt("pool", L1[:], maskLE[:].unsqueeze(1).broadcast_to([128, GT, 128]), gcol4, ALU.mult)
                    for g in range(GT):
                        c = slice(g * 128, (g + 1) * 128)
                        mm(pc[:, c], maskLE[:], L2[:, g, :], start=True, stop=False)
                        mm(pc[:, c], ident_bf[:], negSL[:], start=False, stop=True)
                    for g in range(GT):
                        tsl = slice((t0 + g) * 128, (t0 + g + 1) * 128)
                        mm(pa[:, g * 128:(g + 1) * 128], kh[:, tsl], kh[:, tsl])
                    for g in range(GT):
                        c = slice(g * 128, (g + 1) * 128)
                        mm(pj[0][:, c], L2[:, g, :], maskLE[:], start=True, stop=False)
                        mm(pj[0][:, c], ident_bf[:], negUI[:], start=False, stop=True)
                    for g in range(GT):
                        c = slice(g * 128, (g + 1) * 128)
                        mm(pj[1][:, c], ones_f[:], L1[:, g, :])
                    act(E_sb[:], v3(pc[:, :]), AF.Exp)
                    act(ET_sb[:], v3(pj[0][:, :]), AF.Exp)
                    act(eGB[:], v3(pj[1][:, :]), AF.Exp)
                    tt("dve", E_sb[:], E_sb[:], b4(beta[:, t0:t0 + GT, h:h + 1]), ALU.mult)
                    Ck = CB[0]
                    v4 = lambda p_: p_.rearrange("p (h g c) -> p h g c", h=2, g=HG)
                    CkC = lambda X, g: X[:, g // HG, 0, g % HG, :]
                    CkB = lambda X, g: X[:, g // HG, 1, g % HG, :]
                    tt("dve", Ck[:, :, 0], v4(pa[:, :]), v4(E_sb[:].rearrange("p g c -> p (g c)")), ALU.mult)
                    for g in range(GT):
                        tr(pa[:, g * 128:(g + 1) * 128], CkC(Ck, g), ident[:])
                    for g in range(GT):
                        tsl = slice((t0 + g) * 128, (t0 + g + 1) * 128)
                        c = slice(g * 128, (g + 1) * 128)
                        mm(pb[:, c], kh[:, tsl], qh[:, tsl])
                        tr(pt[:, c], kh[:, tsl], ident_bf[:])
                        tr(pt[:, 512 + g * 128:512 + (g + 1) * 128], vT[:, tsl], ident_bf[:])
                    cp("act", Ck[:, :, 1], v4(pa[:, :]))
                    tt("dve", attnT[:], v3(pb[:, :]), ET_sb[:], ALU.mult)
                    tt("dve", Pg[:], id4, v3(pa[:, :]), ALU.subtract)
                    tt("dve", kbg[:], v3(pt[:, 0:512]), b4(bG[:, t0:t0 + GT, h:h + 1]), ALU.mult)
                    tt("dve", kdec[:], v3(pt[:, 0:512]), b4(ekd[:, t0:t0 + GT, h:h + 1]), ALU.mult)
                    tt("dve", vb[:], v3(pt[:, 512:1024]), b4(beta[:, t0:t0 + GT, h:h + 1]), ALU.mult)
                    qg3 = qh[:, gsl].rearrange("p (g c) -> p g c", g=GT)
                    tt("pool", qg3, qg3, eGB[:], ALU.mult)
                    if tg + 1 < NT // GT:
                        gnx = b4(gsb[:, t0 + GT:t0 + 2 * GT, h:h + 1])
                        tt("pool", L2[:], maskGT[:].unsqueeze(1).broadcast_to([128, GT, 128]), gnx, ALU.mult)
                        tt("pool", L1[:], maskLE[:].unsqueeze(1).broadcast_to([128, GT, 128]), gnx, ALU.mult)
                    bankCB = (pb, pc)
                    bankP = (pj[0], pj[1])
                    for lev in range(1, 8):
                        Cn = CB[lev % 2]
                        for hf in range(2):
                            gs = range(hf * HG, (hf + 1) * HG)
                            pcb = bankCB[hf]
                            pp_ = bankP[hf]
                            if lev <= 6:
                                for gi, g in enumerate(gs):
                                    mm(pcb[:, gi * 128:(gi + 1) * 128], CkB(Ck, g), CkC(Ck, g))
                                if lev <= 5:
                                    for gi, g in enumerate(gs):
                                        mm(pcb[:, 256 + gi * 128:256 + (gi + 1) * 128], CkC(Ck, g), CkB(Ck, g))
                            if lev >= 2:
                                for gi, g in enumerate(gs):
                                    mm(pp_[:, gi * 128:(gi + 1) * 128], CkC(Ck, g), Pg[:, g, :])
                            if lev <= 5:
                                cp("act", Cn[:, hf], pcb[:, :].rearrange("p (a g c) -> p a g c", a=2, g=HG))
                            elif lev == 6:
                                cp("act", Cn[:, hf, 0], v3(pcb[:, 0:256], HG))
                            if lev >= 2:
                                pdst = TT_bf if lev == 7 else Pg
                                tt("dve", pdst[:, hf * HG:(hf + 1) * HG, :], Pg[:, hf * HG:(hf + 1) * HG, :],
                                   v3(pp_[:, 0:256], HG), ALU.add)
                            if pend is not None and 1 <= lev <= GT:
                                (rec_a if hf == 0 else rec_b)(*pend[lev - 1])
                        Ck = Cn
                    for g in range(GT):
                        c = slice(g * 128, (g + 1) * 128)
                        mm(pb[:, c], TT_bf[:, g, :], vb[:, g, :])
                        mm(pc[:, c], kbg[:, g, :], TT_bf[:, g, :])
                    cp("act", u_sb[:], v3(pb[:, :]))
                    cp("dve", wT_sb[:], v3(pc[:, :]))
                    pend = [(h, t0 + g, g, kdec, attnT) for g in range(GT)]
                    if tg == NT // GT - 1:
                        for args in pend:
                            rec_a(*args)
                            rec_b(*args)
                        pend = None
                    if h == 0 and tg == 0:
                        dbg("TT0", TT_bf[:, 0, :])
                def outf_e(tb, rkb, h=h):
                    sl = slice(tb * 512, (tb + 1) * 512)
                    tt("dve", rkb, rkb, raw[:, 3 + tb * 512:3 + (tb + 1) * 512], ALU.mult)
                    stt(oT_all[:, h, sl], rkb, dng[:, 0:1], zs[:, sl], ALU.mult, ALU.mult)
                ep = rms_steps(lambda tb: raw[:, 3 + tb * 512:3 + (tb + 1) * 512], 1.0 / 128.0, outf_e, sil)
                if h < 3:
                    interleave(part_a(h + 1, 0), ep)
                else:
                    interleave(ep)
                if h == 0:
                    dbg("o_raw0", raw[:, 3:3 + T])
                    dbg("oT0", oT_all[:, 0, :])

            dbg("gdn_done", gsb[:])
            outproj(0, True)
            dbg("op0", gsb[:])

            lf = acc[:, 0:NT * 64].rearrange("p (t d) -> p t d", d=64)
            elast = tmp64[0:64].rearrange("p t h -> p (t h)")[:, 0:NT]
            vtm = vT[:].rearrange("p (t d) -> p t d", d=128)
            oTg = raw[:, 3:3 + T]
            for h in range(4):
                qh = qk[0:64, 0, :]
                kh = qk[0:64, 1, :]
                def gate_lf(hh):
                    for half in range(2):
                        p = pj[half]
                        for j in range(8):
                            t = half * 8 + j
                            mm(p[:, j * 64:(j + 1) * 64], grT[:, t * 128:(t + 1) * 128], w2_sb[:, hh * 64:(hh + 1) * 64],
                               start=True, stop=False)
                            mm(p[:, j * 64:(j + 1) * 64], ones_f[0:1, :], gb_sb[0:1, hh * 64:(hh + 1) * 64], start=False, stop=True)
                        act(acc[:, half * 512:(half + 1) * 512], p[:, :], AF.Exp, scale=-1.0)
                    act(acc[:, 0:NT * 64], acc[:, 0:NT * 64], AF.Ln, bias=1.0)

                if h == 0:
                    gate_lf(0)
                    dbg("lf", acc[:, 0:NT * 64])
                def qk_steps(hh):
                    wq = load_w(w_in_l[:, :, 2056 + hh * 64:2056 + (hh + 1) * 64], 64)
                    wk = load_w(w_in_l[:, :, 2312 + hh * 64:2312 + (hh + 1) * 64], 64)

                    def gen():
                        for tb in range(4):
                            sl = slice(tb * 512, (tb + 1) * 512)
                            for j in range(4):
                                t = tb * 4 + j
                                mm(pc[0:64, j * 128:(j + 1) * 128], lf[:, t, :], maskLE[:])
                            act(ebuf[0], pc[0:64, :], AF.Exp, scale=-1.0 / 16.0)
                            act(ebuf[1], pc[0:64, :], AF.Exp, scale=1.0 / 16.0)
                            p = pj[0]
                            for k in range(KC):
                                mm(p[0:64, :], wq[:, k, 0:64], xT[:, k, sl], start=(k == 0), stop=(k == KC - 1))
                            stt(qh[:, sl], p[0:64, :], 0.125, ebuf[0], ALU.mult, ALU.mult)
                            p = pj[1]
                            for k in range(KC):
                                mm(p[0:64, :], wk[:, k, 0:64], xT[:, k, sl], start=(k == 0), stop=(k == KC - 1))
                            tt("dve", kh[:, sl], p[0:64, :], ebuf[1], ALU.mult)
                            for j in range(4):
                                t = tb * 4 + j
                                cp("dve", elast[:, t:t + 1], ebuf[0][:, j * 128 + 127:j * 128 + 128])
                            yield
                    return gen()

                if h == 0:
                    interleave(qk_steps(0))
                wz = load_w(w_in_l[:, :, 3080 + h * 128:3080 + (h + 1) * 128], 128)

                def evz2(tb, p):
                    act(zs[:, tb * 512:(tb + 1) * 512], p, AF.Silu)
                proj_fm(wz, 128, evz2)
                wv = load_w(w_in_l[:, :, 2568 + h * 128:2568 + (h + 1) * 128], 128)
                for t in range(NT):
                    p = pj[t % 2]
                    for k in range(KC):
                        mm(p[:, 0:128], xT[:, k, t * 128:(t + 1) * 128], wv[:, k, :], start=(k == 0), stop=(k == KC - 1))
                    cp("act" if t % 2 == 0 else "dve", vtm[:, t, :], p[:, 0:128])
                if h == 0:
                    dbg("gq", qh)
                    dbg("gk", kh)
                    dbg("gv", vT[:])
                if h == 3:
                    cur["wob"] = outproj_w(1, acc[:].bitcast(BF16)[:, 0:4096])
                m01b = m01UI[:].unsqueeze(1).broadcast_to([128, 4, 128])
                for tg in range(NT // 4):
                    t0 = tg * 4
                    kt4 = kbg if tg % 2 == 0 else vb
                    at4 = attnT2[tg % 2]
                    for g in range(4):
                        tsl = slice((t0 + g) * 128, (t0 + g + 1) * 128)
                        tr(pt[:, g * 64:(g + 1) * 64], kh[:, tsl], ident_bf[0:64, 0:64])
                        mm(pa[:, g * 128:(g + 1) * 128], kh[:, tsl], qh[:, tsl])
                    cp("act", kt4[:, :, 0:64], pt[:, 0:256].rearrange("p (g c) -> p g c", g=4))
                    tt("dve", at4[:], pa[:, :].rearrange("p (g c) -> p g c", g=4), m01b, ALU.mult)
                    for g in range(4):
                        mm(pb[0:64, g * 128:(g + 1) * 128], kt4[:, g, 0:64], vtm[:, t0 + g, :])
                    for g in range(4):
                        t = t0 + g
                        tsl = slice(t * 128, (t + 1) * 128)
                        Rn = Pg[0:64, t % 4, :]
                        Rp = Pg[0:64, (t - 1) % 4, :]
                        Sn = wT_sb[0:64, t % 4, :]
                        Sp = wT_sb[0:64, (t - 1) % 4, :]
                        oslot = po[:, g * 128:(g + 1) * 128]
                        if t == 0:
                            mm(oslot, vtm[:, t, :], at4[:, g, :])
                            cp("dve", Rn, pb[0:64, 0:128])
                        else:
                            mm(oslot, Sp, qh[:, tsl], start=True, stop=False)
                            mm(oslot, vtm[:, t, :], at4[:, g, :], start=False, stop=True)
                            stt(Rn, Rp, elast[:, t - 1:t], pb[0:64, g * 128:(g + 1) * 128], ALU.mult, ALU.add)
                        act(Sn, Rn, AF.Copy, scale=elast[:, t:t + 1])
                    cp("act", oTg[:, t0 * 128:(t0 + 4) * 128], po[:, :])
                if h < 3:
                    gate_lf(h + 1)
                def outf_g(tb, rkb, h=h):
                    sl = slice(tb * 512, (tb + 1) * 512)
                    tt("dve", rkb, rkb, oTg[:, sl], ALU.mult)
                    stt(oT_all[:, h, sl], rkb, glg[:, 0:1], zs[:, sl], ALU.mult, ALU.mult)
                epg = rms_steps(lambda tb: oTg[:, tb * 512:(tb + 1) * 512], 1.0 / 128.0, outf_g,
                                (rk, Pg[:].rearrange("p g c -> p (g c)")))
                if h < 3:
                    interleave(qk_steps(h + 1), epg)
                else:
                    interleave(epg)
                if h == 0:
                    dbg("o_raw4", oTg)
                    dbg("oT4", oT_all[:, 0, :])
            lng_t = CB[0][:].rearrange("p a b g c -> p (a b g c)")
            lnb_t = CB[1][:].rearrange("p a b g c -> p (a b g c)")
            s.dma("sp", lng_t, lng_d[l].partition_broadcast(128))
            s.dma("sp", lnb_t, lnb_d[l].partition_broadcast(128))
            outproj(1, False, ln=(lng_t, lnb_t, last))
            dbg("xout", x_tm[:, 0, :])
          except _Stop:
            s.dma("sp", y_d[0:128, :], x_tm[:, 0, :])
            break
        s.emit()
    return nc, dbg_out


_CACHE = {}


def _prep_inputs(inputs, b):
    f = lambda a: np.ascontiguousarray(np.asarray(a, dtype=np.float32))
    m = {
        "x": f(inputs["x"][b]),
        "w_in": f(inputs["w_in"]),
        "conv_wt": f(np.transpose(np.asarray(inputs["conv_w"]), (0, 2, 1))),
        "dn_a_log": f(inputs["dn_a_log"]),
        "dn_dt_bias": f(inputs["dn_dt_bias"]),
        "gla_gate_w2": f(inputs["gla_gate_w2"]),
        "gla_gate_b": f(inputs["gla_gate_b"]),
        "dn_norm_g": f(inputs["dn_norm_g"]),
        "gla_norm_g": f(inputs["gla_norm_g"]),
        "w_out": f(inputs["w_out"]),
        "ln_g": f(inputs["ln_g"]),
        "ln_b": f(inputs["ln_b"]),
    }
    return m


FUSED = True


def kernel(**inputs):
    in_maps = [_prep_inputs(inputs, b) for b in range(8)]
    groups = [list(range(DEPTH))] if FUSED else [[l] for l in range(DEPTH)]
    for grp in groups:
        key = tuple(grp)
        if key not in _CACHE:
            _CACHE[key] = build(grp)[0]
        res = run_bass_kernel_spmd(_CACHE[key], in_maps, core_ids=list(range(8)))
        ys = [np.asarray(r["y"]) for r in res.results]
        for b in range(8):
            in_maps[b]["x"] = np.ascontiguousarray(ys[b], dtype=np.float32)
    return np.stack(ys, axis=0).astype(np.float32)
```
